# Optimizing a Trainium2 kernel written in Bass

```python
import jax, jax.numpy as jnp
from jax import lax
import numpy as np

D_MODEL = 1024
BATCH = 8
SEQ = 2048
DEPTH = 1
DEC_BATCH = 128
DEC_SEQ = 8
PAST_LEN = 16384
PAGE_SIZE = 128

D_MIX = D_MODEL
HEAD_DIM = 64
D_A = D_MIX // 2
D_B = D_MIX - D_A
N_A_HEADS = D_A // HEAD_DIM
N_Q_HEADS = D_B // HEAD_DIM
N_KV_HEADS = 2
GQA_GROUP = N_Q_HEADS // N_KV_HEADS
KV_DIM = N_KV_HEADS * HEAD_DIM
CHUNK = 128
WINDOW = 128
ROPE_THETA = 10000.0
D_FF = 2816
D_PLE = 256
EPS = 1e-6
D_IN_PROJ = 2 * D_A + D_B + 2 * KV_DIM

kernel_name = "hymba_chunkgmlp_swa_sink_macaron_step"


def _rmsnorm(x, g):
    xf = x.astype(jnp.float32)
    y = xf * lax.rsqrt(jnp.mean(xf * xf, axis=-1, keepdims=True) + EPS)
    return (y * g.astype(jnp.float32)).astype(x.dtype)


def _layernorm(x, g, b):
    xf = x.astype(jnp.float32)
    mu = jnp.mean(xf, axis=-1, keepdims=True)
    xc = xf - mu
    y = xc * lax.rsqrt(jnp.mean(xc * xc, axis=-1, keepdims=True) + EPS)
    return (y * g.astype(jnp.float32) + b.astype(jnp.float32)).astype(x.dtype)


def _swiglu(x, w_gate, w_up, w_down):
    return (jax.nn.silu(x @ w_gate) * (x @ w_up)) @ w_down


def _rope(x, pos):
    half = HEAD_DIM // 2
    inv = ROPE_THETA ** (-jnp.arange(half, dtype=jnp.float32) / half)
    ang = pos.astype(jnp.float32)[:, None] * inv[None, :]
    cos = jnp.cos(ang)[:, None, :]
    sin = jnp.sin(ang)[:, None, :]
    xf = x.astype(jnp.float32)
    x1, x2 = xf[..., :half], xf[..., half:]
    out = jnp.concatenate([x1 * cos - x2 * sin, x2 * cos + x1 * sin], axis=-1)
    return out.astype(x.dtype)


def _chunk_mix(u, v, w_s, b_s):
    n, t = v.shape[:2]
    n_chunks = -(-t // CHUNK)
    pad = n_chunks * CHUNK - t
    vp = jnp.pad(v, ((0, 0), (0, pad), (0, 0), (0, 0)))
    vp = vp.reshape(n, n_chunks, CHUNK, N_A_HEADS, HEAD_DIM)
    causal = jnp.tril(jnp.ones((CHUNK, CHUNK), dtype=bool))
    w = jnp.where(causal[None], w_s, 0)
    mixed = jnp.einsum('hts,ncshd->ncthd', w, vp) + b_s.T[None, None, :, :, None]
    mixed = mixed.reshape(n, n_chunks * CHUNK, N_A_HEADS, HEAD_DIM)[:, :t]
    return u * mixed


def _window_mask(qpos, kpos):
    d = qpos[..., :, None] - kpos[..., None, :]
    return (d >= 0) & (d < WINDOW) & (kpos[..., None, :] >= 0)


def _sink_attention(q, k, v, mask, sinks):
    scale = HEAD_DIM ** -0.5
    s = jnp.einsum('nbqkgd,nbskd->nbkgqs', q, k).astype(jnp.float32) * scale
    s = jnp.where(mask[None, :, None, None], s, -jnp.inf)
    sink = sinks.astype(jnp.float32)[None, None, :, :, None, None]
    m = jnp.maximum(jnp.max(s, axis=-1, keepdims=True), sink)
    p = jnp.exp(s - m)
    denom = jnp.sum(p, axis=-1, keepdims=True) + jnp.exp(sink - m)
    w = (p / denom).astype(v.dtype)
    return jnp.einsum('nbkgqs,nbskd->nbqkgd', w, v)


def _window_attention_prompt(q, k, v, sinks):
    b, s = q.shape[:2]
    nb = -(-s // WINDOW)
    pad = nb * WINDOW - s
    qb = jnp.pad(q, ((0, 0), (0, pad), (0, 0), (0, 0)))
    qb = qb.reshape(b, nb, WINDOW, N_KV_HEADS, GQA_GROUP, HEAD_DIM)
    kb = jnp.pad(k, ((0, 0), (WINDOW, pad), (0, 0), (0, 0))).reshape(b, nb + 1, WINDOW, N_KV_HEADS, HEAD_DIM)
    vb = jnp.pad(v, ((0, 0), (WINDOW, pad), (0, 0), (0, 0))).reshape(b, nb + 1, WINDOW, N_KV_HEADS, HEAD_DIM)
    kband = jnp.concatenate([kb[:, :-1], kb[:, 1:]], axis=2)
    vband = jnp.concatenate([vb[:, :-1], vb[:, 1:]], axis=2)
    blk = jnp.arange(nb)[:, None]
    qpos = blk * WINDOW + jnp.arange(WINDOW)[None, :]
    kpos = (blk - 1) * WINDOW + jnp.arange(2 * WINDOW)[None, :]
    mask = _window_mask(qpos, kpos)
    o = _sink_attention(qb, kband, vband, mask, sinks)
    return o.reshape(b, nb * WINDOW, N_Q_HEADS, HEAD_DIM)[:, :s]


def _window_attention_sample(q, k_new, v_new, k_buf, v_buf, sinks):
    n, t = q.shape[:2]
    buf_len = k_buf.shape[1]
    k_all = jnp.concatenate([k_buf, k_new], axis=1)
    v_all = jnp.concatenate([v_buf, v_new], axis=1)
    qpos = PAST_LEN + jnp.arange(t)
    kpos = PAST_LEN - buf_len + jnp.arange(buf_len + t)
    mask = _window_mask(qpos, kpos)[None]
    qb = q.reshape(n, 1, t, N_KV_HEADS, GQA_GROUP, HEAD_DIM)
    o = _sink_attention(qb, k_all[:, None], v_all[:, None], mask, sinks)
    return o.reshape(n, t, N_Q_HEADS, HEAD_DIM), k_all[:, -buf_len:], v_all[:, -buf_len:]


def _layer(x, pe, pos, lp, k_buf=None, v_buf=None):
    n, t = x.shape[:2]
    x = x + 0.5 * _swiglu(_rmsnorm(x, lp['g_ffn1']), lp['w_ffn1_gate'], lp['w_ffn1_up'], lp['w_ffn1_down'])
    h = _rmsnorm(x, lp['g_mix'])
    z = h @ lp['w_in']
    ua, va, q, k, v = jnp.split(z, [D_A, 2 * D_A, 2 * D_A + D_B, 2 * D_A + D_B + KV_DIM], axis=-1)
    ua = jax.nn.gelu(ua)
    va = _layernorm(jax.nn.gelu(va), lp['g_a_v'], lp['b_a_v'])
    a_out = _chunk_mix(ua.reshape(n, t, N_A_HEADS, HEAD_DIM), va.reshape(n, t, N_A_HEADS, HEAD_DIM),
                       lp['w_s'], lp['b_s']).reshape(n, t, D_A)
    chunk_state = va[:, ((t - 1) // CHUNK) * CHUNK:]
    q = _rope(_rmsnorm(q.reshape(n, t, N_Q_HEADS, HEAD_DIM), lp['g_q']), pos)
    k = _rope(_rmsnorm(k.reshape(n, t, N_KV_HEADS, HEAD_DIM), lp['g_k']), pos)
    v = v.reshape(n, t, N_KV_HEADS, HEAD_DIM)
    sinks = lp['sinks'].reshape(N_KV_HEADS, GQA_GROUP)
    if k_buf is None:
        o = _window_attention_prompt(q, k, v, sinks)
        k_state, v_state = k[:, -WINDOW:], v[:, -WINDOW:]
    else:
        o, k_state, v_state = _window_attention_sample(q, k, v, k_buf, v_buf, sinks)
    b_out = o.reshape(n, t, D_B)
    mix = jnp.concatenate([_rmsnorm(a_out, lp['g_out_a']), _rmsnorm(b_out, lp['g_out_b'])], axis=-1)
    x = x + mix @ lp['w_o']
    x = x + 0.5 * _swiglu(_rmsnorm(x, lp['g_ffn2']), lp['w_ffn2_gate'], lp['w_ffn2_up'], lp['w_ffn2_down'])
    gate = jax.nn.sigmoid(_rmsnorm(x, lp['g_ple']) @ lp['w_ple_gate'])
    x = x + gate * (pe @ lp['w_ple_proj'])
    return x, k_state, v_state, chunk_state


def setup_inputs(seed: int = 0) -> dict:
    key = jax.random.key(seed)
    ks = jax.random.split(key, 32)
    f32 = jnp.float32
    buf_len = min(WINDOW, PAST_LEN)

    def nrm(k, shape, scale=1.0):
        return jax.random.normal(k, shape, f32) * scale

    def gain(k, shape):
        return 1.0 + 0.1 * jax.random.normal(k, shape, f32)

    L = DEPTH
    return {
        "x_prompt": nrm(ks[0], (BATCH, SEQ, D_MODEL)),
        "x_sample": nrm(ks[1], (DEC_BATCH, DEC_SEQ, D_MODEL)),
        "p_prompt": nrm(ks[2], (DEPTH, BATCH, SEQ, D_PLE)),
        "p_sample": nrm(ks[3], (DEPTH, DEC_BATCH, DEC_SEQ, D_PLE)),
        "cache_k_win": nrm(ks[4], (DEPTH, DEC_BATCH, buf_len, N_KV_HEADS, HEAD_DIM)),
        "cache_v_win": nrm(ks[5], (DEPTH, DEC_BATCH, buf_len, N_KV_HEADS, HEAD_DIM)),
        "g_ffn1": gain(ks[6], (L, D_MODEL)),
        "w_ffn1_gate": nrm(ks[7], (L, D_MODEL, D_FF), D_MODEL ** -0.5),
        "w_ffn1_up": nrm(ks[8], (L, D_MODEL, D_FF), D_MODEL ** -0.5),
        "w_ffn1_down": nrm(ks[9], (L, D_FF, D_MODEL), D_FF ** -0.5),
        "g_mix": gain(ks[10], (L, D_MODEL)),
        "w_in": nrm(ks[11], (L, D_MODEL, D_IN_PROJ), D_MODEL ** -0.5),
        "g_a_v": gain(ks[12], (L, D_A)),
        "b_a_v": nrm(ks[13], (L, D_A), 0.1),
        "w_s": nrm(ks[14], (L, N_A_HEADS, CHUNK, CHUNK), CHUNK ** -0.5),
        "b_s": gain(ks[15], (L, N_A_HEADS, CHUNK)),
        "g_q": gain(ks[16], (L, HEAD_DIM)),
        "g_k": gain(ks[17], (L, HEAD_DIM)),
        "sinks": nrm(ks[18], (L, N_Q_HEADS)),
        "g_out_a": gain(ks[19], (L, D_A)),
        "g_out_b": gain(ks[20], (L, D_B)),
        "w_o": nrm(ks[21], (L, D_MIX, D_MODEL), D_MIX ** -0.5),
        "g_ffn2": gain(ks[22], (L, D_MODEL)),
        "w_ffn2_gate": nrm(ks[23], (L, D_MODEL, D_FF), D_MODEL ** -0.5),
        "w_ffn2_up": nrm(ks[24], (L, D_MODEL, D_FF), D_MODEL ** -0.5),
        "w_ffn2_down": nrm(ks[25], (L, D_FF, D_MODEL), D_FF ** -0.5),
        "g_ple": gain(ks[26], (L, D_MODEL)),
        "w_ple_gate": nrm(ks[27], (L, D_MODEL, D_MODEL), D_MODEL ** -0.5),
        "w_ple_proj": nrm(ks[28], (L, D_PLE, D_MODEL), D_PLE ** -0.5),
    }


def reference(x_prompt, x_sample, p_prompt, p_sample, cache_k_win, cache_v_win,
              g_ffn1, w_ffn1_gate, w_ffn1_up, w_ffn1_down, g_mix, w_in, g_a_v, b_a_v,
              w_s, b_s, g_q, g_k, sinks, g_out_a, g_out_b, w_o, g_ffn2, w_ffn2_gate,
              w_ffn2_up, w_ffn2_down, g_ple, w_ple_gate, w_ple_proj):
    pos_p = jnp.arange(x_prompt.shape[1])
    pos_s = PAST_LEN + jnp.arange(x_sample.shape[1])
    yp, ys = x_prompt, x_sample
    pk, pv, pa, sk, sv, sa = [], [], [], [], [], []
    for i in range(DEPTH):
        lp = dict(g_ffn1=g_ffn1[i], w_ffn1_gate=w_ffn1_gate[i], w_ffn1_up=w_ffn1_up[i],
                  w_ffn1_down=w_ffn1_down[i], g_mix=g_mix[i], w_in=w_in[i], g_a_v=g_a_v[i],
                  b_a_v=b_a_v[i], w_s=w_s[i], b_s=b_s[i], g_q=g_q[i], g_k=g_k[i], sinks=sinks[i],
                  g_out_a=g_out_a[i], g_out_b=g_out_b[i], w_o=w_o[i], g_ffn2=g_ffn2[i],
                  w_ffn2_gate=w_ffn2_gate[i], w_ffn2_up=w_ffn2_up[i], w_ffn2_down=w_ffn2_down[i],
                  g_ple=g_ple[i], w_ple_gate=w_ple_gate[i], w_ple_proj=w_ple_proj[i])
        yp, k1, v1, a1 = _layer(yp, p_prompt[i], pos_p, lp)
        ys, k2, v2, a2 = _layer(ys, p_sample[i], pos_s, lp, cache_k_win[i], cache_v_win[i])
        pk.append(k1); pv.append(v1); pa.append(a1)
        sk.append(k2); sv.append(v2); sa.append(a2)
    return (yp, ys, jnp.stack(pk), jnp.stack(pv), jnp.stack(pa),
            jnp.stack(sk), jnp.stack(sv), jnp.stack(sa))
```

```python
import math
from contextlib import ExitStack

import numpy as np
import concourse.bass as bass
import concourse.mybir as mybir
from concourse.bass_utils import run_bass_kernel_spmd

F32 = mybir.dt.float32
BF16 = mybir.dt.bfloat16
I32 = mybir.dt.int32
AF = mybir.ActivationFunctionType
ALU = mybir.AluOpType
AX = mybir.AxisListType

NCORES = 8
NT = 17
TK = NT * 128
DM = 1024
DFF = 2816
NCH = DFF // 128
GROUPS = [(0, 512), (512, 512), (1024, 512), (1536, 512), (2048, 128)]
PIECES = [(0, 5), (5, 10), (10, 15), (15, 20), (20, 22)]
ZOFF = 98560
NS_GU = 3
NS_D = 8
EPS = 1e-6

ENGS = ("pe", "act", "dve", "pool", "sp")


class Res:
    __slots__ = ("name", "d", "lastw", "readers")

    def __init__(self, name, d=False):
        self.name = name
        self.d = d
        self.lastw = None
        self.readers = []


class Op:
    __slots__ = ("eng", "fn", "deps", "tick", "semkey", "is_dma", "signal", "idx")

    def __init__(self, eng, fn, semkey=None):
        self.eng = eng
        self.fn = fn
        self.deps = []
        self.tick = None
        self.semkey = semkey
        self.is_dma = semkey is not None
        self.signal = False
        self.idx = 0


class Tracker:
    def __init__(self):
        self.streams = {e: [] for e in ENGS}
        self.res = {}
        self.barrier = []
        self.cur_last = {}
        self.cur_dma = []
        self.final_dmas = []

    def r(self, name, *idx, d=False):
        key = (name,) + idx
        o = self.res.get(key)
        if o is None:
            o = Res(key, d)
            self.res[key] = o
        return o

    def new_phase(self):
        self.barrier = list(self.cur_last.values()) + list(self.cur_dma)
        self.cur_last = {}
        self.cur_dma = []

    def op(self, eng, fn, reads=(), writes=(), dma=None, final=False):
        o = Op(eng, fn, dma)
        deps = {}

        def add(d, raw):
            if d is None or d is o:
                return
            if (not d.is_dma) and d.eng == eng:
                if eng == "pe":
                    return
            deps[id(d)] = d

        touch_d = False
        read_d = False
        for r in reads:
            touch_d |= r.d
            read_d |= r.d
            add(r.lastw, True)
        for w in writes:
            touch_d |= w.d
            add(w.lastw, False)
            for rd in w.readers:
                add(rd, False)
        if touch_d:
            for b in self.barrier:
                add(b, True)
        for r in reads:
            r.readers.append(o)
        for w in writes:
            w.lastw = o
            w.readers = []
        best = {}
        out = []
        for d in deps.values():
            if d.is_dma:
                out.append(d)
            else:
                b = best.get(d.eng)
                if b is None or d.idx > b.idx:
                    best[d.eng] = d
        out.extend(best.values())
        for d in out:
            d.signal = True
        o.deps = out
        o.idx = len(self.streams[eng])
        self.streams[eng].append(o)
        if touch_d:
            if o.is_dma:
                if read_d:
                    self.cur_dma.append(o)
            else:
                self.cur_last[eng] = o
        if final:
            o.signal = True
            self.final_dmas.append(o)
        return o

    def emit(self, nc, es):
        eng_sem = {}
        for e in ("pe", "act", "dve", "pool"):
            eng_sem[e] = es.enter_context(nc.semaphore("s_" + e))
        dma_sem = {}
        dma_cnt = {}
        for e in ENGS:
            cnt = 0
            for o in self.streams[e]:
                if o.is_dma:
                    if o.semkey not in dma_sem:
                        nm = "d_" + "_".join(str(x) for x in (o.semkey if isinstance(o.semkey, tuple) else (o.semkey,)))
                        dma_sem[o.semkey] = es.enter_context(nc.semaphore(nm))
                        dma_cnt[o.semkey] = 0
                    dma_cnt[o.semkey] += 16
                    o.tick = dma_cnt[o.semkey]
                elif o.signal:
                    assert e != "sp"
                    cnt += 1
                    o.tick = cnt
        block = es.enter_context(nc.Block())
        engobj = {"pe": block.tensor, "act": block.scalar, "dve": block.vector,
                  "pool": block.gpsimd, "sp": block.sync}

        def run(e):
            def body(eng):
                waited = {}
                for o in self.streams[e]:
                    need = {}
                    for d in o.deps:
                        sem = dma_sem[d.semkey] if d.is_dma else eng_sem[d.eng]
                        k = id(sem)
                        if k not in need or need[k][1] < d.tick:
                            need[k] = (sem, d.tick)
                    for k, (sem, tick) in need.items():
                        if waited.get(k, 0) >= tick:
                            continue
                        eng.wait_ge(sem, tick)
                        waited[k] = tick
                    ins = o.fn(eng)
                    if o.is_dma:
                        ins.then_inc(dma_sem[o.semkey], 16)
                    elif o.signal:
                        ins.then_inc(eng_sem[e], 1)
                if e == "sp":
                    last = {}
                    for o in self.final_dmas:
                        last[o.semkey] = max(last.get(o.semkey, 0), o.tick)
                    for key, tick in last.items():
                        sem = dma_sem[key]
                        if waited.get(id(sem), 0) >= tick:
                            continue
                        eng.wait_ge(sem, tick)
                        waited[id(sem)] = tick
            engobj[e](body)

        for e in ENGS:
            run(e)


def build_nc(upto="P", debug=False):
    nc = bass.Bass("TRN2", target_bir_lowering=False)
    DBG = []

    def din(name, shape):
        return nc.dram_tensor(name, list(shape), F32, kind="ExternalInput").ap()

    def dout(name, shape):
        return nc.dram_tensor(name, list(shape), F32, kind="ExternalOutput").ap()

    x_d = din("x", (TK, DM))
    p_d = din("p", (TK, 256))
    ck_d = din("ck", (16, 128, 128))
    cv_d = din("cv", (16, 128, 128))
    w1g_d = din("w1g", (DM, DFF)); w1u_d = din("w1u", (DM, DFF)); w1d_d = din("w1d", (DFF, DM))
    w2g_d = din("w2g", (DM, DFF)); w2u_d = din("w2u", (DM, DFF)); w2d_d = din("w2d", (DFF, DM))
    win_d = din("win", (DM, 1792)); wo_d = din("wo", (DM, DM))
    wpg_d = din("wpg", (DM, DM)); wpp_d = din("wpp", (256, DM))
    g1_d = din("g1", (1, DM)); gm_d = din("gm", (1, DM)); g2_d = din("g2", (1, DM)); gp_d = din("gp", (1, DM))
    gav_d = din("gav", (1, 512)); bav_d = din("bav", (1, 512)); goa_d = din("goa", (1, 512)); gob_d = din("gob", (1, 512))
    gq_d = din("gq", (1, 64)); gk_d = din("gk", (1, 64)); sinks_d = din("sinks", (1, 8))
    ws_d = din("ws", (8, 128, 128)); bs_d = din("bs", (8, 128))
    ident_d = din("ident", (128, 128)); pos_d = din("pos", (128, NT)); invf_d = din("invf", (128, 32))
    mcur_d = din("mcur", (128, 128)); mprev_d = din("mprev", (128, 128))
    mc_d = din("mc", (128, 8)); mnew_d = din("mnew", (128, 128)); sel_d = din("sel", (8, 128))

    y_d = dout("y", (TK, DM))
    pk_d = dout("pk", (128, 128)); pv_d = dout("pv", (128, 128)); pa_d = dout("pa", (128, 512))
    sk_d = dout("sk", (16, 128, 128)); sv_d = dout("sv", (16, 128, 128)); sa_d = dout("sa", (128, 512))

    T = Tracker()
    R = T.r

    def RD(name, *idx):
        return T.r(name, *idx, d=True)

    with ExitStack() as es:
        def sb(name, shape, dt=F32):
            return es.enter_context(nc.sbuf_tensor("sb_" + name, list(shape), dt))

        X = sb("X", (128, NT, DM))
        identf = sb("identf", (128, 128))
        identb = sb("identb", (128, 128), BF16)
        G = [sb("G0", (128, DM)), sb("G1", (128, DM))]
        CS = sb("CS", (128, 2, NT, 32))
        mcur = sb("mcur", (128, 128), BF16); mprev = sb("mprev", (128, 128), BF16)
        mnew = sb("mnew", (128, 128), BF16); mc = sb("mc", (128, 8), BF16)
        ssq = sb("ssq", (128, NT)); rstd = sb("rstd", (128, NT)); neghalf = sb("neghalf", (128, NT))
        st = sb("st", (128, 64))
        bnst = sb("bnst", (128, 6))
        esink = sb("esink", (128, 8))
        posf = sb("posf", (128, NT)); invf = sb("invf", (128, 32))
        dbytes = (nc.sbuf_bytes_remaining - 1024) // 4 * 4
        assert dbytes >= 126000, dbytes
        Dreg = sb("Dreg", (128, dbytes // 4))
        ps = [es.enter_context(nc.psum_tensor(f"ps{i}", [128, 512], F32)) for i in range(8)]
        psb = [p[:, :].bitcast(BF16) for p in ps]

        class Lay:
            def __init__(self):
                self.off = 0

            def take(self, nbytes, dt=F32):
                assert self.off % 4 == 0
                a = self.off // 4
                n = (nbytes + 3) // 4
                self.off += n * 4
                assert self.off <= dbytes, (self.off, dbytes)
                v = Dreg[:, a:a + n]
                if dt != F32:
                    v = v.bitcast(dt)
                return v

        T.op("sp", lambda e: e.dma_start(out=identf[:], in_=ident_d), writes=[R("identf")], dma="c_ident")
        T.op("sp", lambda e: e.dma_start(out=posf[:], in_=pos_d), writes=[R("posf")], dma="c_pos")
        T.op("sp", lambda e: e.dma_start(out=invf[:], in_=invf_d), writes=[R("invf")], dma="c_invf")
        for nm, dst, src in (("mcur", mcur, mcur_d), ("mprev", mprev, mprev_d), ("mnew", mnew, mnew_d), ("mc", mc, mc_d)):
            T.op("pool", lambda e, dst=dst, src=src: e.dma_start(out=dst[:], in_=src), writes=[R(nm)], dma="c_" + nm)
        T.op("dve", lambda e: e.tensor_copy(out=identb[:], in_=identf[:]), reads=[R("identf")], writes=[R("identb")])
        T.op("dve", lambda e: e.memset(neghalf[:], -0.5), writes=[R("neghalf")])
        def rope_tables():
            lay = Lay()
            lay.off = 110 * 1024
            NA = 2 * NT * 32
            ANG = lay.take(NA * 4)
            KI = lay.take(NA * 4, I32)
            KF = lay.take(NA * 4)
            ang3 = ANG.rearrange("p (c t j) -> p c t j", c=2, t=NT)
            C1 = 6.28125
            C2 = 2.0 * math.pi - 6.28125
            NH = NT * 32
            SA = ANG[:, NH:2 * NH]
            CA = ANG[:, 0:NH]
            KIh = KI[:, 0:NH]
            KFh = KF[:, 0:NH]
            T.op("dve", lambda e: e.tensor_tensor(out=ang3[:, 1], in0=posf[:, :].unsqueeze(2).to_broadcast([128, NT, 32]),
                                                  in1=invf[:, :].unsqueeze(1).to_broadcast([128, NT, 32]), op=ALU.mult),
                 reads=[R("posf"), R("invf")], writes=[R("ANG")])
            T.op("dve", lambda e: e.tensor_scalar(out=KIh, in0=SA, scalar1=1.0 / (2 * math.pi), scalar2=None, op0=ALU.mult),
                 reads=[R("ANG")], writes=[R("KI")])
            T.op("dve", lambda e: e.tensor_copy(out=KFh, in_=KIh), reads=[R("KI")], writes=[R("KF")])
            T.op("dve", lambda e: e.scalar_tensor_tensor(out=SA, in0=KFh, scalar=-C1, in1=SA, op0=ALU.mult, op1=ALU.add),
                 reads=[R("KF"), R("ANG")], writes=[R("ANG")])
            T.op("dve", lambda e: e.scalar_tensor_tensor(out=SA, in0=KFh, scalar=-C2, in1=SA, op0=ALU.mult, op1=ALU.add),
                 reads=[R("KF"), R("ANG")], writes=[R("ANG")])
            T.op("dve", lambda e: e.tensor_scalar(out=CA, in0=SA, scalar1=math.pi / 2, scalar2=None, op0=ALU.add),
                 reads=[R("ANG")], writes=[R("ANG")])
            T.op("dve", lambda e: e.tensor_scalar(out=KF, in0=ANG, scalar1=math.pi, scalar2=-2 * math.pi, op0=ALU.is_gt, op1=ALU.mult),
                 reads=[R("ANG")], writes=[R("KF")])
            T.op("dve", lambda e: e.tensor_tensor(out=ANG, in0=ANG, in1=KF, op=ALU.add), reads=[R("ANG"), R("KF")], writes=[R("ANG")])
            T.op("dve", lambda e: e.tensor_scalar(out=KF, in0=ANG, scalar1=math.pi, scalar2=-2 * math.pi, op0=ALU.is_gt, op1=ALU.mult),
                 reads=[R("ANG")], writes=[R("KF")])
            T.op("dve", lambda e: e.tensor_tensor(out=ANG, in0=ANG, in1=KF, op=ALU.add), reads=[R("ANG"), R("KF")], writes=[R("ANG")])
            T.op("dve", lambda e: e.tensor_scalar(out=KF, in0=ANG, scalar1=-math.pi, scalar2=2 * math.pi, op0=ALU.is_lt, op1=ALU.mult),
                 reads=[R("ANG")], writes=[R("KF")])
            T.op("dve", lambda e: e.tensor_tensor(out=ANG, in0=ANG, in1=KF, op=ALU.add), reads=[R("ANG"), R("KF")], writes=[R("ANG")])
            T.op("dve", lambda e: e.tensor_scalar(out=ANG, in0=ANG, scalar1=-3.14159, scalar2=3.14159, op0=ALU.max, op1=ALU.min),
                 reads=[R("ANG")], writes=[R("ANG")])
            T.op("act", lambda e: e.activation(out=CS[:].rearrange("p c t j -> p (c t j)"), in_=ANG, func=AF.Sin),
                 reads=[R("ANG")], writes=[R("CS")])

            T.op("sp", lambda e: e.dma_start(out=sk_d[:, 0:120, :], in_=ck_d[:, 8:128, :]), dma="o_sk0", final=True)
            T.op("sp", lambda e: e.dma_start(out=sv_d[:, 0:120, :], in_=cv_d[:, 8:128, :]), dma="o_sv0", final=True)


        gcount = [0]

        def load_gain(g_dram):
            gs = gcount[0] % 2
            gcount[0] += 1
            T.op("sp", lambda e: e.dma_start(out=G[gs][:], in_=g_dram.partition_broadcast(128)),
                 writes=[R("G", gs)], dma=("G", gs))
            return gs

        def rstd_ops(src, dst, nh, scale, eps, rkeys_in, rkeys_out):
            T.op("dve", lambda e: e.tensor_scalar(out=dst, in0=src, scalar1=scale, scalar2=eps, op0=ALU.mult, op1=ALU.add),
                 reads=rkeys_in, writes=rkeys_out)
            T.op("pool", lambda e: e.tensor_tensor(out=dst, in0=dst, in1=nh, op=ALU.pow),
                 reads=rkeys_out + [R("neghalf")], writes=rkeys_out)

        layz = Lay()
        layz.off = ZOFF
        WIN = layz.take(8 * 1792 * 2, BF16).rearrange("p (k c) -> p k c", k=8)
        layz = Lay()
        layz.off = ZOFF
        WPG = layz.take(8 * DM * 2, BF16).rearrange("p (k c) -> p k c", k=8)
        WPP = layz.take(2 * DM * 2, BF16).rearrange("p (k c) -> p k c", k=2)
        WINR = [R("WIN", i) for i in range(10)]

        def prefetch_M():
            win3 = win_d.rearrange("(k p) c -> p k c", p=128)
            T.op("pool", lambda e: e.dma_start(out=WIN[:, :, 0:1024], in_=win3[:, :, 0:1024]), reads=[R("CS")], writes=[R("WIN", 0)], dma="w_in0")
            for kh in range(2):
                for b in range(4):
                    T.op("pool", lambda e, kh=kh, b=b: e.dma_start(
                        out=WIN[:, :, 1024 + b * 128 + kh * 64:1024 + b * 128 + kh * 64 + 64],
                        in_=win3[:, :, 1024 + (kh * 4 + b) * 64:1024 + (kh * 4 + b) * 64 + 64]),
                        reads=[R("CS")], writes=[R("WIN", 1 + kh * 4 + b)], dma=("w_in1", kh * 4 + b))
            T.op("pool", lambda e: e.dma_start(out=WIN[:, :, 1536:1792], in_=win3[:, :, 1536:1792]), reads=[R("CS")], writes=[R("WIN", 9)], dma="w_in3")

        def prefetch_P():
            T.op("pool", lambda e: e.dma_start(out=WPG[:], in_=wpg_d.rearrange("(k p) c -> p k c", p=128)), writes=[R("WPG")] + WINR, dma="w_pg")
            T.op("pool", lambda e: e.dma_start(out=WPP[:], in_=wpp_d.rearrange("(k p) c -> p k c", p=128)), writes=[R("WPP")] + WINR, dma="w_pp")

        def ffn_phase(tag, g_dram, Wg, Wu, Wd, have_stats=False, after_prologue=None, first=False):
            T.new_phase()
            lay = Lay()
            XNT = lay.take(8 * TK * 2, BF16).rearrange("p (k t) -> p k t", k=8)
            PMAX = max(b - a for a, b in PIECES)
            HT = lay.take(PMAX * TK * 2, BF16).rearrange("p (j t) -> p j t", j=PMAX)
            WGs = [lay.take(8 * 128 * 2, BF16).rearrange("p (k c) -> p k c", k=8) for _ in range(NS_GU)]
            WUs = [lay.take(8 * 128 * 2, BF16).rearrange("p (k c) -> p k c", k=8) for _ in range(NS_GU)]
            WDs = [lay.take(DM * 2, BF16) for _ in range(NS_D)]
            SG = [lay.take(512 * 4) for _ in range(2)]
            HNB = [lay.take(DM * 2, BF16) for _ in range(2)]
            JUNK = lay.take(DM * 2, BF16)
            assert lay.off <= ZOFF, lay.off
            gs = load_gain(g_dram)
            if first:
                for t in range(NT):
                    T.op("sp", lambda e, t=t: e.dma_start(out=X[:, t, :], in_=x_d[t * 128:(t + 1) * 128, :]),
                         writes=[R("X", t)], dma=("x", t))
            Wg3 = Wg.rearrange("(k p) c -> p k c", p=128)
            Wu3 = Wu.rearrange("(k p) c -> p k c", p=128)

            def load_gu(j):
                s = j % NS_GU
                T.op("pool", lambda e: e.dma_start(out=WGs[s][:], in_=Wg3[:, :, j * 128:(j + 1) * 128]),
                     writes=[RD("WG", s)], dma=("wg", s))
                T.op("pool", lambda e: e.dma_start(out=WUs[s][:], in_=Wu3[:, :, j * 128:(j + 1) * 128]),
                     writes=[RD("WU", s)], dma=("wu", s))

            def load_d(j):
                s = j % NS_D
                T.op("pool", lambda e: e.dma_start(out=WDs[s], in_=Wd[j * 128:(j + 1) * 128, :]),
                     writes=[RD("WD", s)], dma=("wd", s))

            load_gu(0)
            load_gu(1)
            pending_loads = [lambda: load_gu(2)]

            acnt = [0]

            def emitA(j, gi):
                jj = j - [p0 for (p0, p1) in PIECES if p0 <= j < p1][0]
                s = j % NS_GU
                c0, n = GROUPS[gi]
                tiles = list(range(c0 // 128, (c0 + n) // 128))
                pair = acnt[0] % 2
                acnt[0] += 1
                bg, bu = 2 * pair, 2 * pair + 1

                def mmA(e):
                    for k in range(8):
                        e.matmul(ps[bg][:, 0:n], lhsT=WGs[s][:, k, :], rhs=XNT[:, k, c0:c0 + n], start=(k == 0), stop=(k == 7))
                    for k in range(8):
                        ins = e.matmul(ps[bu][:, 0:n], lhsT=WUs[s][:, k, :], rhs=XNT[:, k, c0:c0 + n], start=(k == 0), stop=(k == 7))
                    return ins
                T.op("pe", mmA, reads=[RD("WG", s), RD("WU", s)] + [RD("XNT", t) for t in tiles],
                     writes=[R("ps", bg), R("ps", bu)])
                T.op("act", lambda e: e.activation(out=SG[pair][:, 0:n], in_=ps[bg][:, 0:n], func=AF.Silu),
                     reads=[R("ps", bg)], writes=[RD("SG", pair)])
                T.op("dve", lambda e: e.tensor_tensor(out=HT[:, jj, c0:c0 + n], in0=SG[pair][:, 0:n], in1=ps[bu][:, 0:n], op=ALU.mult),
                     reads=[RD("SG", pair), R("ps", bu)], writes=[RD("HT", jj, gi)])

            def sq_group(gi):
                c0, n = GROUPS[gi]
                for t in range(c0 // 128, (c0 + n) // 128):
                    T.op("act", lambda e, t=t: e.activation(out=JUNK, in_=X[:, t, :], func=AF.Square, accum_out=ssq[:, t:t + 1]),
                         reads=[R("X", t)], writes=[RD("JUNK"), R("ssq", t)])

            if not have_stats:
                sq_group(0)
                sq_group(1)
            for gi, (c0, n) in enumerate(GROUPS):
                tiles = list(range(c0 // 128, (c0 + n) // 128))
                t0, t1 = tiles[0], tiles[-1] + 1
                if not have_stats:
                    rstd_ops(ssq[:, t0:t1], rstd[:, t0:t1], neghalf[:, t0:t1], 1.0 / DM, EPS,
                             [R("ssq", t) for t in tiles], [R("rstd", t) for t in tiles])
                for t in tiles:
                    hb = HNB[t % 2]
                    T.op("dve", lambda e, t=t, hb=hb: e.scalar_tensor_tensor(out=hb, in0=X[:, t, :], scalar=rstd[:, t:t + 1], in1=G[gs][:],
                                                                             op0=ALU.mult, op1=ALU.mult),
                         reads=[R("X", t), R("rstd", t), R("G", gs)], writes=[RD("HNB", t % 2)])
                    bank = 4 + t % 4

                    def tr(e, hb=hb, bank=bank):
                        for k in range(8):
                            ins = e.transpose(out=psb[bank][:, k * 128:(k + 1) * 128], in_=hb[:, k * 128:(k + 1) * 128], identity=identb[:])
                        return ins
                    T.op("pe", tr, reads=[RD("HNB", t % 2), R("identb")], writes=[R("ps", bank)])
                    if t % 2 == 0:
                        T.op("act", lambda e, t=t, bank=bank: e.activation(out=XNT[:, :, t * 128:(t + 1) * 128],
                                                                          in_=psb[bank][:, :].rearrange("p (k c) -> p k c", k=8), func=AF.Copy),
                             reads=[R("ps", bank)], writes=[RD("XNT", t)])
                    else:
                        T.op("dve", lambda e, t=t, bank=bank: e.tensor_copy(out=XNT[:, :, t * 128:(t + 1) * 128],
                                                                           in_=psb[bank][:, :].rearrange("p (k c) -> p k c", k=8)),
                             reads=[R("ps", bank)], writes=[RD("XNT", t)])
                if gi + 2 < len(GROUPS) and not have_stats:
                    sq_group(gi + 2)
                if pending_loads:
                    pending_loads.pop(0)()
                if gi >= 1:
                    emitA(0, gi - 1)
            emitA(0, len(GROUPS) - 1)

            if first:
                rope_tables()
            for j in range(NS_D):
                load_d(j)
            bcnt = 0
            for (j0, j1) in PIECES:
                for j in range(j0, j1):
                    if j > 0:
                        for gi in range(len(GROUPS)):
                            emitA(j, gi)
                    if j + NS_GU < NCH:
                        load_gu(j + NS_GU)
                npj = j1 - j0
                for t in range(NT):
                    gi = min(t // 4, 4)
                    for hf in range(2):
                        bank = 4 + bcnt % 4
                        bcnt += 1

                        def mmB(e, t=t, hf=hf, bank=bank, j0=j0, npj=npj):
                            for jj in range(npj):
                                ins = e.matmul(ps[bank][:, :], lhsT=HT[:, jj, t * 128:(t + 1) * 128],
                                               rhs=WDs[(j0 + jj) % NS_D][:, hf * 512:(hf + 1) * 512], start=(jj == 0), stop=(jj == npj - 1))
                            return ins
                        T.op("pe", mmB, reads=[RD("HT", jj, gi) for jj in range(npj)] + [RD("WD", (j0 + jj) % NS_D) for jj in range(npj)],
                             writes=[R("ps", bank)])
                        T.op("dve", lambda e, t=t, hf=hf, bank=bank: e.scalar_tensor_tensor(out=X[:, t, hf * 512:(hf + 1) * 512], in0=ps[bank][:, :], scalar=0.5,
                                                                                          in1=X[:, t, hf * 512:(hf + 1) * 512], op0=ALU.mult, op1=ALU.add),
                             reads=[R("ps", bank), R("X", t)], writes=[R("X", t)])
                    if j1 == NCH:
                        T.op("act", lambda e, t=t: e.activation(out=JUNK, in_=X[:, t, :], func=AF.Square, accum_out=ssq[:, t:t + 1]),
                             reads=[R("X", t)], writes=[RD("JUNK"), R("ssq", t)])
                if j1 == NCH:
                    rstd_ops(ssq[:, :], rstd[:, :], neghalf[:, :], 1.0 / DM, EPS, [R("ssq", t) for t in range(NT)], [R("rstd", t) for t in range(NT)])
                for j in range(j0 + NS_D, min(j1 + NS_D, NCH)):
                    load_d(j)
                if j0 == 0 and after_prologue is not None:
                    after_prologue()

        def mix_phase(ntiles=NT, laststep=99):
            T.new_phase()
            lay = Lay()
            WO = lay.take(8 * DM * 2, BF16).rearrange("p (k c) -> p k c", k=8)
            GAV = lay.take(2048); BAV = lay.take(2048); GOA = lay.take(2048); GOB = lay.take(2048)
            GQK = lay.take(640 * 4)
            GQ1 = lay.take(64 * 4); GK1 = lay.take(64 * 4)
            WST = lay.take(8 * 128 * 2, BF16).rearrange("p (h t) -> p h t", h=8)
            WBD = lay.take(8 * 128 * 2, BF16).rearrange("p (h t) -> p h t", h=8)
            BST = lay.take(8 * 4); BSS = lay.take(8 * 4)
            SELB = lay.take(128 * 2, BF16)
            KT = lay.take(NT * 128 * 2, BF16).rearrange("p (t s) -> p t s", t=NT)
            VAUG = lay.take(NT * 130 * 2, BF16).rearrange("p (t k d) -> p t k d", t=NT, k=2)
            KCT = lay.take(16 * 128 * 2, BF16).rearrange("p (n s) -> p n s", n=16)
            VC = lay.take(16 * 130 * 2, BF16).rearrange("p (n k d) -> p n k d", n=16, k=2)
            KCraw = lay.take(16 * 128 * 2)
            KC = KCraw.bitcast(BF16).rearrange("p (n c) -> p n c", n=16)
            HN = lay.take(DM * 2, BF16)
            HNT = lay.take(DM * 2, BF16)
            JUNKM = lay.take(DM * 2, BF16)
            PTraw = lay.take(DM * 4, BF16)
            PT = PTraw.rearrange("p (i c) -> p i c", i=4)
            PTC = PTraw[:, 1024:2048]
            GEL = lay.take(DM * 4)
            WSF = GEL.rearrange("p (h s) -> p h s", h=8)
            VAB = lay.take(512 * 2, BF16)
            SQQK = lay.take(640 * 4)
            QKH = lay.take(640 * 4)
            RT = lay.take(640 * 4)
            QR = lay.take(640 * 2, BF16)
            QTs = [lay.take(512 * 2, BF16).rearrange("p (b t) -> p b t", b=4) for _ in range(2)]
            OB = lay.take(2048)
            A2b = lay.take(2048)
            TMP = lay.take(2048)
            MIXs = [lay.take(DM * 2, BF16) for _ in range(2)]
            MIXT = lay.take(DM * 2, BF16)
            KFo = lay.take(512); VFo = lay.take(512); VAF = lay.take(2048)
            OTS = KCraw.rearrange("p (k c) -> p k c", k=2)

            assert lay.off <= ZOFF, lay.off
            if debug:
                for nm, ap_, res_ in (("HN", HN, RD("HN")), ("OB", OB, RD("OB")), ("KT", KT.rearrange("p t s -> p (t s)"), RD("KT", 0))):
                    DBG.append((nm, ap_, res_))
            gs = load_gain(gm_d)
            T.op("pool", lambda e: e.dma_start(out=WO[:], in_=wo_d.rearrange("(k p) c -> p k c", p=128)), writes=[RD("WO")], dma="w_o")
            T.op("pool", lambda e: e.dma_start(out=KC[:], in_=ck_d.rearrange("n s c -> s n c")), writes=[RD("KC")], dma="c_kc")
            for kh in range(2):
                T.op("pool", lambda e, kh=kh: e.dma_start(out=VC[:, :, kh, 0:64], in_=cv_d[:, :, kh * 64:(kh + 1) * 64].rearrange("n s d -> s n d")),
                     writes=[RD("VC", kh)], dma=("c_vc", kh))
            for nm, dst, src in (("GAV", GAV, gav_d), ("BAV", BAV, bav_d), ("GOA", GOA, goa_d), ("GOB", GOB, gob_d),
                                 ("GQ1", GQ1, gq_d), ("GK1", GK1, gk_d)):
                T.op("sp", lambda e, dst=dst, src=src: e.dma_start(out=dst, in_=src.partition_broadcast(128)), writes=[RD(nm)], dma="c_" + nm)
            T.op("sp", lambda e: e.dma_start(out=esink[:], in_=sinks_d.partition_broadcast(128)), writes=[R("esink")], dma="c_sink")
            T.op("sp", lambda e: e.dma_start(out=WSF, in_=ws_d.rearrange("h t s -> t h s")), writes=[RD("GEL", 0), RD("GEL", 1)], dma="c_wsf")
            T.op("sp", lambda e: e.dma_start(out=BST, in_=bs_d.rearrange("h t -> t h"), allow_slow_non_contiguous=True), writes=[RD("BST")], dma="c_bst")
            T.op("pool", lambda e: e.dma_start(out=SELB[0:8, :], in_=sel_d), writes=[RD("SELB")], dma="c_sel")
            for n in range(16):
                T.op("sp", lambda e, n=n: e.dma_start(out=BSS[8 * n:8 * n + 8, :], in_=bs_d[:, 0:8].rearrange("h i -> i h"), allow_slow_non_contiguous=True),
                     writes=[RD("BSS", n)], dma=("c_bss", n % 2))
            BSSR = [RD("BSS", n) for n in range(16)]
            T.op("act", lambda e: e.activation(out=esink[:], in_=esink[:], func=AF.Exp), reads=[R("esink")], writes=[R("esink")])
            T.op("dve", lambda e: e.tensor_copy(out=GQK[:, 0:512].rearrange("p (h d) -> p h d", h=8), in_=GQ1.unsqueeze(1).to_broadcast([128, 8, 64])),
                 reads=[RD("GQ1")], writes=[RD("GQK")])
            T.op("dve", lambda e: e.tensor_copy(out=GQK[:, 512:640].rearrange("p (h d) -> p h d", h=2), in_=GK1.unsqueeze(1).to_broadcast([128, 2, 64])),
                 reads=[RD("GK1")], writes=[RD("GQK")])
            T.op("dve", lambda e: e.memset(VAUG[:, :, :, 64:65], 1.0), writes=[RD("VAUG1")])
            T.op("dve", lambda e: e.memset(VC[:, :, :, 64:65], 1.0), writes=[RD("VC1")])
            for half in range(2):
                def trw(e, half=half):
                    for hh in range(4):
                        ins = e.transpose(out=ps[5 + half][:, hh * 128:(hh + 1) * 128], in_=WSF[:, half * 4 + hh, :], identity=identf[:])
                    return ins
                T.op("pe", trw, reads=[RD("GEL", 0), RD("GEL", 1), R("identf")], writes=[R("ps", 5 + half)])
                T.op("dve", lambda e, half=half: e.tensor_tensor(out=WST[:, half * 4:half * 4 + 4, :], in0=ps[5 + half][:, :].rearrange("p (h t) -> p h t", h=4),
                                                                 in1=mcur[:, :].unsqueeze(1).to_broadcast([128, 4, 128]), op=ALU.mult),
                     reads=[R("ps", 5 + half), R("mcur")], writes=[RD("WST")])
            T.op("pe", lambda e: e.matmul(ps[7][:, 0:64], lhsT=SELB[0:8, :], rhs=WST[0:8, :, 0:8], start=True, stop=True),
                 reads=[RD("SELB"), RD("WST")], writes=[R("ps", 7)])
            T.op("dve", lambda e: e.tensor_tensor(out=WBD[:, :, :].rearrange("p h (n i) -> p h n i", n=16),
                                                  in0=ps[7][:, 0:64].rearrange("p (h i) -> p h i", h=8).unsqueeze(2).to_broadcast([128, 8, 16, 8]),
                                                  in1=mnew[:, :].rearrange("p (n i) -> p n i", n=16).unsqueeze(1).to_broadcast([128, 8, 16, 8]), op=ALU.mult),
                 reads=[R("ps", 7), R("mnew")], writes=[RD("WBD")])
            for half in range(2):
                def trk(e, half=half):
                    for nn in range(8):
                        ins = e.transpose(out=psb[3 + half][:, nn * 128:(nn + 1) * 128], in_=KC[:, half * 8 + nn, :], identity=identb[:])
                    return ins
                T.op("pe", trk, reads=[RD("KC"), R("identb")], writes=[R("ps", 3 + half)])
                T.op("act", lambda e, half=half: e.activation(out=KCT[:, half * 8:half * 8 + 8, :], in_=psb[3 + half][:, :].rearrange("p (n s) -> p n s", n=8), func=AF.Copy),
                     reads=[R("ps", 3 + half)], writes=[RD("KCT")])

            HNT3 = HNT.rearrange("p (k t) -> p k t", k=8)
            MIXT3 = MIXT.rearrange("p (k t) -> p k t", k=8)
            q3 = QKH.rearrange("p (h d) -> p h d", h=10)
            x1, x2 = q3[:, :, 0:32], q3[:, :, 32:64]
            t1 = SQQK[:, 0:320].rearrange("p (h d) -> p h d", h=10)
            t3 = SQQK[:, 320:640].rearrange("p (h d) -> p h d", h=10)
            t2 = RT[:, 0:320].rearrange("p (h d) -> p h d", h=10)
            t4 = RT[:, 320:640].rearrange("p (h d) -> p h d", h=10)
            qr3 = QR.rearrange("p (h d) -> p h d", h=10)
            VP = GEL[:, 512:1024]

            def A1(t):
                T.op("dve", lambda e: e.scalar_tensor_tensor(out=HN, in0=X[:, t, :], scalar=rstd[:, t:t + 1], in1=G[gs][:], op0=ALU.mult, op1=ALU.mult),
                     reads=[R("X", t), R("rstd", t), R("G", gs)], writes=[RD("HN")])

            def ZA(t, part=2):
                def tr1(e):
                    for k in range(8):
                        ins = e.transpose(out=psb[0][:, k * 128:(k + 1) * 128], in_=HN[:, k * 128:(k + 1) * 128], identity=identb[:])
                    return ins
                if part in (0, 2):
                    T.op("pe", tr1, reads=[RD("HN"), R("identb")], writes=[R("ps", 0)])
                if part in (1, 2):
                    T.op("act", lambda e: e.activation(out=HNT, in_=psb[0][:, :], func=AF.Copy), reads=[R("ps", 0)], writes=[RD("HNT")])

            def ZB(t, part=2):
                def mmz(e, groups):
                    for c, a, b in groups:
                        for k in range(8):
                            ins = e.matmul(ps[1 + c][:, 0:b - a], lhsT=HNT3[:, k, :], rhs=WIN[:, k, a:b], start=(k == 0), stop=(k == 7))
                    return ins
                if part in (0, 2):
                    T.op("pe", lambda e: mmz(e, ((2, 1024, 1536), (3, 1536, 1792))), reads=[RD("HNT")] + WINR, writes=[R("ps", 3), R("ps", 4), R("ps4d")])
                if part in (1, 2):
                    T.op("pe", lambda e: mmz(e, ((0, 0, 512), (1, 512, 1024))), reads=[RD("HNT")] + WINR, writes=[R("ps", 1), R("ps", 2)])

            def B1g(t):
                for c in range(2):
                    T.op("act", lambda e, c=c: e.activation(out=GEL[:, c * 512:(c + 1) * 512], in_=ps[1 + c][:, :], func=AF.Gelu_apprx_tanh),
                         reads=[R("ps", 1 + c)], writes=[RD("GEL", c)])

            def B1a(t):
                S = (t == NT - 1)
                outt = t >= NT - 2
                T.op("act", lambda e: e.activation(out=SQQK[:, 0:512], in_=ps[3][:, :], func=AF.Square), reads=[R("ps", 3)], writes=[RD("SQQK"), RD("SQQK2")])
                T.op("act", lambda e: e.activation(out=SQQK[:, 512:640], in_=ps[4][:, 0:128], func=AF.Square), reads=[R("ps", 4)], writes=[RD("SQQK2")])
                T.op("dve", lambda e: e.tensor_reduce(out=st[:, 8:18], in_=SQQK.rearrange("p (h d) -> p h d", h=10), axis=AX.X, op=ALU.add),
                     reads=[RD("SQQK"), RD("SQQK2")], writes=[R("st_sh")])
                rstd_ops(st[:, 8:18], st[:, 20:30], neghalf[:, 0:10], 1.0 / 64, EPS, [R("st_sh")], [R("st_rh")])
                T.op("act", lambda e: e.activation(out=VAUG[:, t, :, 0:64], in_=ps[4][:, 128:256].rearrange("p (k d) -> p k d", k=2), func=AF.Copy),
                     reads=[R("ps", 4)], writes=[RD("VAUG", t)])
                if outt:
                    T.op("act", lambda e: e.activation(out=VFo, in_=ps[4][:, 128:256], func=AF.Copy), reads=[R("ps", 4)], writes=[RD("VFo")])
                    if S:
                        T.op("sp", lambda e: e.dma_start(out=sv_d[:, 120:128, :], in_=VFo), reads=[RD("VFo")], dma="o_sv1", final=True)
                    else:
                        T.op("sp", lambda e: e.dma_start(out=pv_d, in_=VFo), reads=[RD("VFo")], dma="o_pv", final=True)

            def B1a2(t):
                T.op("dve", lambda e: e.tensor_tensor(out=QKH[:, 0:512].rearrange("p (h d) -> p h d", h=8), in0=ps[3][:, :].rearrange("p (h d) -> p h d", h=8),
                                                      in1=st[:, 20:28].unsqueeze(2).to_broadcast([128, 8, 64]), op=ALU.mult),
                     reads=[R("ps", 3), R("st_rh")], writes=[RD("QKH")])
                T.op("dve", lambda e: e.tensor_tensor(out=QKH[:, 512:640].rearrange("p (h d) -> p h d", h=2), in0=ps[4][:, 0:128].rearrange("p (h d) -> p h d", h=2),
                                                      in1=st[:, 28:30].unsqueeze(2).to_broadcast([128, 2, 64]), op=ALU.mult),
                     reads=[R("ps", 4), R("st_rh")], writes=[RD("QKH")])

            def B1b(t):
                S = (t == NT - 1)
                outt = t >= NT - 2
                T.op("dve", lambda e: e.tensor_tensor(out=QKH, in0=QKH, in1=GQK, op=ALU.mult), reads=[RD("QKH"), RD("GQK")], writes=[RD("QKH")])
                cosb = CS[:, 0, t, :].unsqueeze(1).to_broadcast([128, 10, 32])
                sinb = CS[:, 1, t, :].unsqueeze(1).to_broadcast([128, 10, 32])
                T.op("pool", lambda e: e.tensor_tensor(out=t2, in0=x2, in1=sinb, op=ALU.mult), reads=[RD("QKH"), R("CS")], writes=[RD("RT")])
                T.op("pool", lambda e: e.tensor_tensor(out=t4, in0=x1, in1=sinb, op=ALU.mult), reads=[RD("QKH"), R("CS")], writes=[RD("RT2")])
                T.op("dve", lambda e: e.tensor_tensor(out=t1, in0=x1, in1=cosb, op=ALU.mult), reads=[RD("QKH"), R("CS")], writes=[RD("SQQK")])
                T.op("dve", lambda e: e.tensor_tensor(out=t3, in0=x2, in1=cosb, op=ALU.mult), reads=[RD("QKH"), R("CS")], writes=[RD("SQQK2")])
                T.op("dve", lambda e: e.tensor_tensor(out=qr3[:, :, 0:32], in0=t1, in1=t2, op=ALU.subtract), reads=[RD("SQQK"), RD("RT")], writes=[RD("QR")])
                T.op("dve", lambda e: e.tensor_tensor(out=qr3[:, :, 32:64], in0=t3, in1=t4, op=ALU.add), reads=[RD("SQQK2"), RD("RT2")], writes=[RD("QR")])
                if outt:
                    kf3 = KFo.rearrange("p (h d) -> p h d", h=2)
                    T.op("pool", lambda e: e.tensor_tensor(out=kf3[:, :, 0:32], in0=t1[:, 8:10, :], in1=t2[:, 8:10, :], op=ALU.subtract),
                         reads=[RD("SQQK"), RD("RT")], writes=[RD("KFo")])
                    T.op("pool", lambda e: e.tensor_tensor(out=kf3[:, :, 32:64], in0=t3[:, 8:10, :], in1=t4[:, 8:10, :], op=ALU.add),
                         reads=[RD("SQQK2"), RD("RT2")], writes=[RD("KFo")])
                    if S:
                        T.op("sp", lambda e: e.dma_start(out=sk_d[:, 120:128, :], in_=KFo), reads=[RD("KFo")], dma="o_sk1", final=True)
                    else:
                        T.op("sp", lambda e: e.dma_start(out=pk_d, in_=KFo), reads=[RD("KFo")], dma="o_pk", final=True)
                T.op("dve", lambda e: e.bn_stats(out=bnst[:], in_=VP), reads=[RD("GEL", 1)], writes=[R("bnst")])
                T.op("dve", lambda e: e.bn_aggr(out=st[:, 0:2], in_=bnst[:]), reads=[R("bnst")], writes=[R("st_mv")])
                rstd_ops(st[:, 1:2], st[:, 2:3], neghalf[:, 0:1], 1.0, EPS, [R("st_mv")], [R("st_rv")])
                T.op("dve", lambda e: e.tensor_scalar(out=VP, in0=VP, scalar1=st[:, 0:1], scalar2=st[:, 2:3], op0=ALU.subtract, op1=ALU.mult),
                     reads=[RD("GEL", 1), R("st_mv"), R("st_rv")], writes=[RD("GEL", 1)])
                T.op("dve", lambda e: e.tensor_tensor(out=VP, in0=VP, in1=GAV, op=ALU.mult), reads=[RD("GEL", 1), RD("GAV")], writes=[RD("GEL", 1)])
                T.op("dve", lambda e: e.tensor_tensor(out=VAB, in0=VP, in1=BAV, op=ALU.add), reads=[RD("GEL", 1), RD("BAV")], writes=[RD("VAB")])
                if outt:
                    T.op("dve", lambda e: e.tensor_tensor(out=VAF, in0=VP, in1=BAV, op=ALU.add), reads=[RD("GEL", 1), RD("BAV")], writes=[RD("VAF")])
                    T.op("sp", lambda e: e.dma_start(out=(sa_d if S else pa_d), in_=VAF), reads=[RD("VAF")], dma=("o_va", int(S)), final=True)

            def B2(t):
                S = (t == NT - 1)
                qb = t % 2
                mb = t % 2

                def tr2(e):
                    for b in range(5):
                        ins = e.transpose(out=psb[0][:, b * 128:(b + 1) * 128], in_=QR[:, b * 128:(b + 1) * 128], identity=identb[:])
                    return ins
                T.op("pe", tr2, reads=[RD("QR"), R("identb")], writes=[R("ps", 0)])
                T.op("act", lambda e: e.activation(out=QTs[qb][:, :, :], in_=psb[0][:, 0:512].rearrange("p (b t) -> p b t", b=4), func=AF.Copy),
                     reads=[R("ps", 0)], writes=[RD("QT", qb)])
                T.op("act", lambda e: e.activation(out=KT[:, t, :], in_=psb[0][:, 512:640], func=AF.Copy), reads=[R("ps", 0)], writes=[RD("KT", t)])
                WSX = WBD if S else WST

                def mmc(e):
                    for h in range(8):
                        ins = e.matmul(ps[7][:, h * 64:(h + 1) * 64], lhsT=WSX[:, h, :], rhs=VAB[:, h * 64:(h + 1) * 64],
                                       start=(h == 0), stop=(h == 7), skip_group_check=True)
                    return ins
                T.op("pe", mmc, reads=[RD("WBD") if S else RD("WST"), RD("VAB")], writes=[R("ps", 7)])
                BSX = BSS if S else BST
                T.op("dve", lambda e: e.tensor_tensor(out=TMP.rearrange("p (h d) -> p h d", h=8), in0=ps[7][:, :].rearrange("p (h d) -> p h d", h=8),
                                                      in1=BSX.unsqueeze(2).to_broadcast([128, 8, 64]), op=ALU.add),
                     reads=[R("ps", 7)] + (BSSR if S else [RD("BST")]), writes=[RD("TMP")])
                T.op("dve", lambda e: e.tensor_tensor(out=A2b, in0=TMP, in1=GEL[:, 0:512], op=ALU.mult), reads=[RD("TMP"), RD("GEL", 0)], writes=[RD("A2")])

            def B2c(t):
                mb = t % 2
                T.op("act", lambda e: e.activation(out=MIXs[mb][:, 0:512], in_=A2b, func=AF.Square, accum_out=st[:, 32:33]), reads=[RD("A2")], writes=[RD("MIXa", mb), R("st_sa")])
                rstd_ops(st[:, 32:33], st[:, 33:34], neghalf[:, 0:1], 1.0 / 512, EPS, [R("st_sa")], [R("st_ra")])
                T.op("dve", lambda e: e.scalar_tensor_tensor(out=MIXs[mb][:, 0:512], in0=A2b, scalar=st[:, 33:34], in1=GOA, op0=ALU.mult, op1=ALU.mult),
                     reads=[RD("A2"), R("st_ra"), RD("GOA")], writes=[RD("MIXa", mb)])

            def C1(t, khs=(0, 1)):
                qb = t % 2
                blocks = ([(t - 1, mprev, "mprev")] if t > 0 else []) + [(t, mcur, "mcur")]
                for kh in khs:
                    for bi, (tk, mk, mkn) in enumerate(blocks):
                        bank = 5 + bi
                        idx = kh * 2 + bi
                        T.op("pe", lambda e, kh=kh, tk=tk, bank=bank: e.matmul(ps[bank][:, :], lhsT=KT[64 * kh:64 * kh + 64, tk, :],
                                                                                rhs=QTs[qb][64 * kh:64 * kh + 64, :, :], start=True, stop=True),
                             reads=[RD("KT", tk), RD("QT", qb)], writes=[R("ps", bank)])
                        T.op("act", lambda e, idx=idx, bank=bank: e.activation(out=PT[:, idx, :], in_=ps[bank][:, :], func=AF.Exp, scale=0.125),
                             reads=[R("ps", bank)], writes=[RD("PT", idx)])
                        T.op("dve", lambda e, idx=idx, mk=mk: e.tensor_tensor(out=PT[:, idx, :].rearrange("p (b t) -> p b t", b=4),
                                                                             in0=PT[:, idx, :].rearrange("p (b t) -> p b t", b=4),
                                                                             in1=mk[:, :].unsqueeze(1).to_broadcast([128, 4, 128]), op=ALU.mult),
                             reads=[RD("PT", idx), R(mkn)], writes=[RD("PT", idx)])

            def C2(t):
                mb = t % 2
                blocks = ([(t - 1,)] if t > 0 else []) + [(t,)]

                def mmpv(e):
                    first = True
                    for h in range(8):
                        kh, b = h // 4, h % 4
                        for bi, (tk,) in enumerate(blocks):
                            ins = e.matmul(ps[7][:, h * 64:(h + 1) * 64], lhsT=PT[:, kh * 2 + bi, b * 128:(b + 1) * 128],
                                           rhs=VAUG[:, tk, kh, 0:64], start=first, stop=False, skip_group_check=True)
                            first = False
                    first = True
                    for h in range(8):
                        kh, b = h // 4, h % 4
                        for bi, (tk,) in enumerate(blocks):
                            ins = e.matmul(ps[4][:, 256 + h:257 + h], lhsT=PT[:, kh * 2 + bi, b * 128:(b + 1) * 128],
                                           rhs=VAUG[:, tk, kh, 64:65], start=first, stop=False, skip_group_check=True)
                            first = False
                    return ins
                T.op("pe", mmpv, reads=[RD("PT", i) for i in range(4)] + [RD("VAUG1")] + [RD("VAUG", tk) for (tk,) in blocks], writes=[R("ps", 7), R("ps4d"), R("ps", 4)])
                T.op("dve", lambda e: e.tensor_tensor(out=st[:, 40:48], in0=ps[4][:, 256:264], in1=esink[:, :], op=ALU.add),
                     reads=[R("ps4d"), R("esink")], writes=[R("st_den")])
                T.op("dve", lambda e: e.reciprocal(out=st[:, 48:56], in_=st[:, 40:48]), reads=[R("st_den")], writes=[R("st_rden")])
                T.op("dve", lambda e: e.tensor_tensor(out=OB.rearrange("p (h d) -> p h d", h=8), in0=ps[7][:, :].rearrange("p (h d) -> p h d", h=8),
                                                      in1=st[:, 48:56].unsqueeze(2).to_broadcast([128, 8, 64]), op=ALU.mult),
                     reads=[R("ps", 7), R("st_rden")], writes=[RD("OB")])
                fin_b(t)

            def fin_b(t):
                mb = t % 2
                T.op("act", lambda e: e.activation(out=MIXs[mb][:, 512:1024], in_=OB, func=AF.Square, accum_out=st[:, 34:35]), reads=[RD("OB")], writes=[RD("MIXb", mb), R("st_sb")])
                rstd_ops(st[:, 34:35], st[:, 35:36], neghalf[:, 0:1], 1.0 / 512, EPS, [R("st_sb")], [R("st_rb")])
                T.op("dve", lambda e: e.scalar_tensor_tensor(out=MIXs[mb][:, 512:1024], in0=OB, scalar=st[:, 35:36], in1=GOB, op0=ALU.mult, op1=ALU.mult),
                     reads=[RD("OB"), R("st_rb"), RD("GOB")], writes=[RD("MIXb", mb)])

            def CSamp(t):
                qb = t % 2
                QTq = QTs[qb]

                def mmsc(e):
                    for n in range(16):
                        for kh in range(2):
                            ins = e.matmul(ps[1 + kh][:, n * 32:n * 32 + 32],
                                           lhsT=KCT[64 * kh:64 * kh + 64, n, :], rhs=QTq[64 * kh:64 * kh + 64, :, 8 * n:8 * n + 8],
                                           start=(n == 0), stop=False, skip_group_check=True)
                    return ins
                T.op("pe", mmsc, reads=[RD("KCT"), RD("QT", qb)], writes=[R("ps", 1), R("ps", 2)])
                for half in range(2):
                    T.op("act", lambda e, half=half: e.activation(out=PTC[:, half * 512:(half + 1) * 512], in_=ps[1 + half][:, :], func=AF.Exp, scale=0.125),
                         reads=[R("ps", 1 + half)], writes=[RD("PT", 2 + half)])
                T.op("dve", lambda e: e.tensor_tensor(out=PTC.rearrange("p (a i) -> p a i", i=8), in0=PTC.rearrange("p (a i) -> p a i", i=8),
                                                      in1=mc[:, :].unsqueeze(1).to_broadcast([128, 128, 8]), op=ALU.mult),
                     reads=[RD("PT", 2), RD("PT", 3), R("mc")], writes=[RD("PT", 2), RD("PT", 3)])
                for kh in range(2):
                    T.op("pe", lambda e, kh=kh: e.matmul(ps[3 + kh][:, :], lhsT=KT[64 * kh:64 * kh + 64, t, :],
                                                         rhs=QTq[64 * kh:64 * kh + 64, :, :].rearrange("p b (n i) -> p n b i", n=16), start=True, stop=True),
                         reads=[RD("KT", t), RD("QT", qb)], writes=[R("ps", 3 + kh)])
                    T.op("act", lambda e, kh=kh: e.activation(out=PT[:, kh, :], in_=ps[3 + kh][:, :], func=AF.Exp, scale=0.125),
                         reads=[R("ps", 3 + kh)], writes=[RD("PT", kh)])
                    T.op("dve", lambda e, kh=kh: e.tensor_tensor(out=PT[:, kh, :].rearrange("p (n b i) -> p n b i", n=16, b=4), in0=PT[:, kh, :].rearrange("p (n b i) -> p n b i", n=16, b=4),
                                                                in1=mnew[:, :].rearrange("p (n i) -> p n i", n=16).unsqueeze(2).to_broadcast([128, 16, 4, 8]), op=ALU.mult),
                         reads=[RD("PT", kh), R("mnew")], writes=[RD("PT", kh)])
                PTC5 = PTC.rearrange("p (k n b i) -> p k n b i", n=16, k=2, b=4)

                def mmot(e):
                    for kh in range(2):
                        for n in range(16):
                            e.matmul(ps[1 + kh][0:65, n * 32:(n + 1) * 32], lhsT=VC[:, n, kh, :], rhs=PTC[:, kh * 512 + n * 32:kh * 512 + (n + 1) * 32],
                                     start=(n == 0), stop=False, skip_group_check=True)
                        ins = e.matmul(ps[1 + kh][0:65, :], lhsT=VAUG[:, t, kh, :], rhs=PT[:, kh, :], start=False, stop=True, skip_group_check=True)
                    return ins
                T.op("pe", mmot, reads=[RD("PT", i) for i in range(4)] + [RD("VC", 0), RD("VC", 1), RD("VC1"), RD("VAUG", t), RD("VAUG1")], writes=[R("ps", 1), R("ps", 2)])
                for kh in range(2):
                    T.op("act", lambda e, kh=kh: e.activation(out=OTS[0:65, kh, :].rearrange("p (b n i) -> p b n i", b=4, n=16),
                                                              in_=ps[1 + kh][0:65, :].rearrange("p (n b i) -> p b n i", n=16, b=4), func=AF.Copy),
                         reads=[R("ps", 1 + kh)], writes=[RD("KC")])

                def trot(e):
                    for bank in range(2):
                        for hh in range(4):
                            h = bank * 4 + hh
                            kh, b = h // 4, h % 4
                            ins = e.transpose(out=ps[5 + bank][:, hh * 65:(hh + 1) * 65], in_=OTS[0:65, kh, b * 128:(b + 1) * 128], identity=identf[0:65, 0:65])
                    return ins
                T.op("pe", trot, reads=[RD("KC"), R("identf")], writes=[R("ps", 5), R("ps", 6)])
                for bank in range(2):
                    o3 = ps[5 + bank][:, 0:260].rearrange("p (h d) -> p h d", h=4)
                    T.op("dve", lambda e, bank=bank, o3=o3: e.tensor_tensor(out=st[:, 40 + bank * 4:44 + bank * 4].unsqueeze(2), in0=o3[:, :, 64:65],
                                                                            in1=esink[:, bank * 4:bank * 4 + 4].unsqueeze(2), op=ALU.add),
                         reads=[R("ps", 5 + bank), R("esink")], writes=[R("st_den")])
                T.op("dve", lambda e: e.reciprocal(out=st[:, 48:56], in_=st[:, 40:48]), reads=[R("st_den")], writes=[R("st_rden")])
                for bank in range(2):
                    o3 = ps[5 + bank][:, 0:260].rearrange("p (h d) -> p h d", h=4)
                    T.op("dve", lambda e, bank=bank, o3=o3: e.tensor_tensor(out=OB[:, bank * 256:(bank + 1) * 256].rearrange("p (h d) -> p h d", h=4), in0=o3[:, :, 0:64],
                                                                            in1=st[:, 48 + bank * 4:52 + bank * 4].unsqueeze(2).to_broadcast([128, 4, 64]), op=ALU.mult),
                         reads=[R("ps", 5 + bank), R("st_rden")], writes=[RD("OB")])
                fin_b(t)

            def D1(t):
                mb = t % 2

                def tr3(e):
                    for k in range(8):
                        ins = e.transpose(out=psb[0][:, k * 128:(k + 1) * 128], in_=MIXs[mb][:, k * 128:(k + 1) * 128], identity=identb[:])
                    return ins
                T.op("pe", tr3, reads=[RD("MIXa", mb), RD("MIXb", mb), R("identb")], writes=[R("ps", 0)])
                T.op("act", lambda e: e.activation(out=MIXT, in_=psb[0][:, :], func=AF.Copy), reads=[R("ps", 0)], writes=[RD("MIXT")])

            def D1b(t):
                for hf in range(2):
                    def mmo(e, hf=hf):
                        for k in range(8):
                            ins = e.matmul(ps[5 + hf][:, :], lhsT=MIXT3[:, k, :], rhs=WO[:, k, hf * 512:(hf + 1) * 512], start=(k == 0), stop=(k == 7))
                        return ins
                    T.op("pe", mmo, reads=[RD("MIXT"), RD("WO")], writes=[R("ps", 5 + hf)])

            def D2(t):
                for hf in range(2):
                    T.op("dve", lambda e, hf=hf: e.tensor_tensor(out=X[:, t, hf * 512:(hf + 1) * 512], in0=ps[5 + hf][:, :], in1=X[:, t, hf * 512:(hf + 1) * 512], op=ALU.add),
                         reads=[R("ps", 5 + hf), R("X", t)], writes=[R("X", t)])

            def F2stats(t):
                T.op("act", lambda e: e.activation(out=JUNKM, in_=X[:, t, :], func=AF.Square, accum_out=ssq[:, t:t + 1]),
                     reads=[R("X", t)], writes=[RD("JUNKM"), R("ssq", t)])

            if ntiles > 0:
                A1(0)
                ZA(0)
                ZB(0)
                if ntiles > 1:
                    A1(1)
                B1a(0)
                B1a2(0)
            for i in range(ntiles + 1):
                cur = i < ntiles
                prev = i >= 1
                nxt = i + 1 < ntiles
                if cur and i >= 1:
                    B1a2(i)
                if prev and (i - 1) != NT - 1:
                    C1(i - 1, (0,))
                    if nxt:
                        ZA(i + 1, 0)
                    C1(i - 1, (1,))
                    if nxt:
                        ZA(i + 1, 1)
                elif nxt:
                    ZA(i + 1)
                if prev:
                    B2c(i - 1)
                if cur:
                    B1g(i)
                if i >= 2:
                    F2stats(i - 2)
                if prev:
                    if (i - 1) == NT - 1:
                        CSamp(i - 1)
                    else:
                        C2(i - 1)
                if nxt:
                    ZB(i + 1, 0)
                if cur:
                    B1b(i)
                if prev:
                    D1(i - 1)
                if nxt:
                    ZB(i + 1, 1)
                if prev:
                    D1b(i - 1)
                    D2(i - 1)
                if nxt:
                    B1a(i + 1)
                if cur:
                    B2(i)
                if i + 2 < ntiles:
                    A1(i + 2)
            if ntiles >= 1:
                F2stats(ntiles - 1)
            if ntiles == NT:
                rstd_ops(ssq[:, :], rstd[:, :], neghalf[:, :], 1.0 / DM, EPS, [R("ssq", t) for t in range(NT)], [R("rstd", t) for t in range(NT)])

        def ple_phase():
            T.new_phase()
            lay = Lay()
            PET = lay.take(2 * TK * 2, BF16).rearrange("p (k t) -> p k t", k=2)
            PST = lay.take(NT * 256 * 2, BF16).rearrange("p (t c) -> p t c", t=NT)
            HN = [lay.take(DM * 2, BF16) for _ in range(2)]
            HNT = [lay.take(DM * 2, BF16) for _ in range(2)]
            JUNK = lay.take(DM * 2, BF16)
            SIG = [lay.take(DM * 4) for _ in range(2)]
            TP = [lay.take(DM * 4) for _ in range(2)]
            assert lay.off <= ZOFF, lay.off
            gs = load_gain(gp_d)
            T.op("pool", lambda e: e.dma_start(out=PST[:], in_=p_d.rearrange("(t q) c -> q t c", q=128)), writes=[RD("PST")], dma="c_pst")
            for t in range(NT):
                bank = t % 2

                def trp(e, t=t, bank=bank):
                    for kk in range(2):
                        ins = e.transpose(out=psb[bank][:, kk * 128:(kk + 1) * 128], in_=PST[:, t, kk * 128:(kk + 1) * 128], identity=identb[:])
                    return ins
                T.op("pe", trp, reads=[RD("PST"), R("identb")], writes=[R("ps", bank)])
                T.op("act", lambda e, t=t, bank=bank: e.activation(out=PET[:, :, t * 128:(t + 1) * 128], in_=psb[bank][:, 0:256].rearrange("p (k c) -> p k c", k=2), func=AF.Copy),
                     reads=[R("ps", bank)], writes=[RD("PET", t)])
            cnt = [0]

            def PA1n(t):
                b2 = t % 2
                T.op("dve", lambda e: e.scalar_tensor_tensor(out=HN[b2], in0=X[:, t, :], scalar=rstd[:, t:t + 1], in1=G[gs][:], op0=ALU.mult, op1=ALU.mult),
                     reads=[R("X", t), R("rstd", t), R("G", gs)], writes=[RD("HN", b2)])

            def PA1t(t):
                b2 = t % 2

                def tr1(e):
                    for k in range(8):
                        ins = e.transpose(out=psb[b2][:, k * 128:(k + 1) * 128], in_=HN[b2][:, k * 128:(k + 1) * 128], identity=identb[:])
                    return ins
                T.op("pe", tr1, reads=[RD("HN", b2), R("identb")], writes=[R("ps", b2)])

            def PA2(t):
                b2 = t % 2
                T.op("act", lambda e: e.activation(out=HNT[b2], in_=psb[b2][:, :], func=AF.Copy), reads=[R("ps", b2)], writes=[RD("HNT", b2)])

            def PB(t):
                b2 = t % 2
                H3 = HNT[b2].rearrange("p (k t) -> p k t", k=8)
                for hf in range(2):
                    pr = cnt[0] % 3
                    cnt[0] += 1
                    bgate, bproj = 2 + 2 * pr, 3 + 2 * pr

                    def mmg(e, hf=hf, bgate=bgate, bproj=bproj):
                        for k in range(8):
                            e.matmul(ps[bgate][:, :], lhsT=H3[:, k, :], rhs=WPG[:, k, hf * 512:(hf + 1) * 512], start=(k == 0), stop=(k == 7))
                        for kk in range(2):
                            ins = e.matmul(ps[bproj][:, :], lhsT=PET[:, kk, t * 128:(t + 1) * 128], rhs=WPP[:, kk, hf * 512:(hf + 1) * 512], start=(kk == 0), stop=(kk == 1))
                        return ins
                    T.op("pe", mmg, reads=[RD("HNT", b2), R("WPG"), R("WPP"), RD("PET", t)], writes=[R("ps", bgate), R("ps", bproj)])
                    T.op("act", lambda e, hf=hf, bgate=bgate: e.activation(out=SIG[b2][:, hf * 512:(hf + 1) * 512], in_=ps[bgate][:, :], func=AF.Sigmoid),
                         reads=[R("ps", bgate)], writes=[RD("SIG", b2, hf)])
                    T.op("dve", lambda e, hf=hf, bproj=bproj: e.tensor_tensor(out=TP[b2][:, hf * 512:(hf + 1) * 512], in0=SIG[b2][:, hf * 512:(hf + 1) * 512],
                                                                          in1=ps[bproj][:, :], op=ALU.mult),
                         reads=[RD("SIG", b2, hf), R("ps", bproj)], writes=[RD("TP", b2, hf)])
                T.op("dve", lambda e: e.tensor_tensor(out=X[:, t, :], in0=X[:, t, :], in1=TP[b2], op=ALU.add),
                     reads=[R("X", t), RD("TP", b2, 0), RD("TP", b2, 1)], writes=[R("X", t)])
                T.op("sp", lambda e: e.dma_start(out=y_d[t * 128:(t + 1) * 128, :], in_=X[:, t, :]), reads=[R("X", t)], dma="o_y", final=True)

            PA1n(0)
            PA1t(0)
            PA2(0)
            PA1n(1)
            PA1t(1)
            PA1n(2)
            PA1t(2)
            for t in range(NT):
                if t + 1 < NT:
                    PA2(t + 1)
                if t + 3 < NT:
                    PA1n(t + 3)
                PB(t)
                if t + 3 < NT:
                    PA1t(t + 3)

        def store_y():
            for t in range(NT):
                T.op("sp", lambda e, t=t: e.dma_start(out=y_d[t * 128:(t + 1) * 128, :], in_=X[:, t, :]), reads=[R("X", t)], dma="o_y", final=True)

        ffn_phase("f1", g1_d, w1g_d, w1u_d, w1d_d, after_prologue=prefetch_M, first=True)
        if upto == "F1":
            store_y()
        elif upto.startswith("M:"):
            _, a_, b_ = upto.split(":")
            mix_phase(int(a_), float(b_))
            store_y()
        else:
            mix_phase()
            if upto == "M":
                store_y()
            else:
                ffn_phase("f2", g2_d, w2g_d, w2u_d, w2d_d, have_stats=True, after_prologue=prefetch_P)
                if upto == "F2":
                    store_y()
                else:
                    ple_phase()
        if debug:
            for nm, ap_, res_ in DBG:
                dd = nc.dram_tensor("dbg_" + nm, list(ap_.shape), ap_.dtype, kind="ExternalOutput").ap()
                T.op("sp", lambda e, dd=dd, ap_=ap_: e.dma_start(out=dd, in_=ap_), reads=[res_], dma="o_dbg", final=True)
            dst = nc.dram_tensor("dbg_st", [128, 64], F32, kind="ExternalOutput").ap()
            T.op("sp", lambda e: e.dma_start(out=dst, in_=st[:]), reads=[R("st_rden"), R("st_rb"), R("st_ra"), R("st_rh"), R("st_rv"), R("st_mv")], dma="o_dbg", final=True)
            dcs = nc.dram_tensor("dbg_CS", [128, 2 * NT * 32], F32, kind="ExternalOutput").ap()
            T.op("sp", lambda e: e.dma_start(out=dcs, in_=CS[:].rearrange("p c t j -> p (c t j)")), reads=[R("CS")], dma="o_dbg", final=True)
        T.emit(nc, es)
    return nc


def _consts():
    ident = np.eye(128, dtype=np.float32)
    pos = np.zeros((128, NT), np.float32)
    for t in range(16):
        pos[:, t] = t * 128 + np.arange(128)
    pos[:, 16] = 16384 + (np.arange(128) % 8)
    half = 32
    invf = (10000.0 ** (-np.arange(half, dtype=np.float32) / half)).astype(np.float32)
    invf = np.broadcast_to(invf[None, :], (128, 32)).copy()
    s = np.arange(128)[:, None]
    q = np.arange(128)[None, :]
    mcur = (s <= q).astype(np.float32)
    mprev = (s > q).astype(np.float32)
    mc = (s > np.arange(8)[None, :]).astype(np.float32)
    mnew = ((s // 8 == q // 8) & (s % 8 <= q % 8)).astype(np.float32)
    sel = (np.arange(128)[None, :] % 8 == np.arange(8)[:, None]).astype(np.float32)
    return dict(ident=ident, pos=pos, invf=invf, mcur=mcur, mprev=mprev, mc=mc, mnew=mnew, sel=sel)


def make_in_maps(inp):
    f = lambda a: np.ascontiguousarray(np.asarray(a, dtype=np.float32))
    shared = dict(
        w1g=f(inp["w_ffn1_gate"][0]), w1u=f(inp["w_ffn1_up"][0]), w1d=f(inp["w_ffn1_down"][0]),
        w2g=f(inp["w_ffn2_gate"][0]), w2u=f(inp["w_ffn2_up"][0]), w2d=f(inp["w_ffn2_down"][0]),
        win=f(inp["w_in"][0]), wo=f(inp["w_o"][0]), wpg=f(inp["w_ple_gate"][0]), wpp=f(inp["w_ple_proj"][0]),
        g1=f(inp["g_ffn1"]), gm=f(inp["g_mix"]), g2=f(inp["g_ffn2"]), gp=f(inp["g_ple"]),
        gav=f(inp["g_a_v"]), bav=f(inp["b_a_v"]), goa=f(inp["g_out_a"]), gob=f(inp["g_out_b"]),
        gq=f(inp["g_q"]), gk=f(inp["g_k"]), sinks=f(inp["sinks"]),
        ws=f(inp["w_s"][0]), bs=f(inp["b_s"][0]),
    )
    shared.update(_consts())
    xp, xs = f(inp["x_prompt"]), f(inp["x_sample"])
    pp, psm = f(inp["p_prompt"][0]), f(inp["p_sample"][0])
    ck, cv = f(inp["cache_k_win"][0]), f(inp["cache_v_win"][0])
    maps = []
    for c in range(NCORES):
        sl = slice(16 * c, 16 * c + 16)
        m = dict(shared)
        m["x"] = np.concatenate([xp[c], xs[sl].reshape(128, DM)], axis=0)
        m["p"] = np.concatenate([pp[c], psm[sl].reshape(128, 256)], axis=0)
        m["ck"] = ck[sl].reshape(16, 128, 128)
        m["cv"] = cv[sl].reshape(16, 128, 128)
        maps.append(m)
    return maps


_NC_CACHE = {}


def kernel(**inputs):
    if "nc" not in _NC_CACHE:
        _NC_CACHE["nc"] = build_nc("P")
    nc = _NC_CACHE["nc"]
    maps = make_in_maps(inputs)
    res = run_bass_kernel_spmd(nc, maps, core_ids=list(range(NCORES)))
    rs = res.results
    y = np.stack([r["y"] for r in rs])
    yp = np.ascontiguousarray(y[:, 0:2048, :])
    ys = np.ascontiguousarray(y[:, 2048:, :].reshape(128, 8, DM))
    pk = np.stack([r["pk"] for r in rs]).reshape(1, 8, 128, 2, 64)
    pv = np.stack([r["pv"] for r in rs]).reshape(1, 8, 128, 2, 64)
    pa = np.stack([r["pa"] for r in rs]).reshape(1, 8, 128, 512)
    sk = np.concatenate([r["sk"] for r in rs]).reshape(1, 128, 128, 2, 64)
    sv = np.concatenate([r["sv"] for r in rs]).reshape(1, 128, 128, 2, 64)
    sa = np.concatenate([r["sa"] for r in rs]).reshape(1, 128, 8, 512)
    return (yp, ys, pk.astype(np.float32), pv.astype(np.float32), pa.astype(np.float32),
            sk.astype(np.float32), sv.astype(np.float32), sa.astype(np.float32))
```

```python
import math
from contextlib import ExitStack

import numpy as np
import concourse.bass as bass
import concourse.mybir as mybir
from concourse.bass_utils import run_bass_kernel_spmd

F32 = mybir.dt.float32
BF16 = mybir.dt.bfloat16
I32 = mybir.dt.int32
AF = mybir.ActivationFunctionType
ALU = mybir.AluOpType
AX = mybir.AxisListType

NCORES = 8
NT = 17
TK = NT * 128
DM = 1024
DFF = 2816
NCH = DFF // 128
GROUPS = [(0, 512), (512, 512), (1024, 512), (1536, 512), (2048, 128)]
PIECES = [(0, 5), (5, 10), (10, 15), (15, 20), (20, 22)]
ZOFF = 98560
NS_GU = 3
NS_D = 8
EPS = 1e-6

ENGS = ("pe", "act", "dve", "pool", "sp")


class Res:
    __slots__ = ("name", "d", "lastw", "readers")

    def __init__(self, name, d=False):
        self.name = name
        self.d = d
        self.lastw = None
        self.readers = []


class Op:
    __slots__ = ("eng", "fn", "deps", "tick", "semkey", "is_dma", "signal", "idx")

    def __init__(self, eng, fn, semkey=None):
        self.eng = eng
        self.fn = fn
        self.deps = []
        self.tick = None
        self.semkey = semkey
        self.is_dma = semkey is not None
        self.signal = False
        self.idx = 0


class Tracker:
    def __init__(self):
        self.streams = {e: [] for e in ENGS}
        self.res = {}
        self.barrier = []
        self.cur_last = {}
        self.cur_dma = []
        self.final_dmas = []

    def r(self, name, *idx, d=False):
        key = (name,) + idx
        o = self.res.get(key)
        if o is None:
            o = Res(key, d)
            self.res[key] = o
        return o

    def new_phase(self):
        self.barrier = list(self.cur_last.values()) + list(self.cur_dma)
        self.cur_last = {}
        self.cur_dma = []

    def op(self, eng, fn, reads=(), writes=(), dma=None, final=False):
        o = Op(eng, fn, dma)
        deps = {}

        def add(d, raw):
            if d is None or d is o:
                return
            if (not d.is_dma) and d.eng == eng:
                if eng == "pe":
                    return
            deps[id(d)] = d

        touch_d = False
        read_d = False
        for r in reads:
            touch_d |= r.d
            read_d |= r.d
            add(r.lastw, True)
        for w in writes:
            touch_d |= w.d
            add(w.lastw, False)
            for rd in w.readers:
                add(rd, False)
        if touch_d:
            for b in self.barrier:
                add(b, True)
        for r in reads:
            r.readers.append(o)
        for w in writes:
            w.lastw = o
            w.readers = []
        best = {}
        out = []
        for d in deps.values():
            if d.is_dma:
                out.append(d)
            else:
                b = best.get(d.eng)
                if b is None or d.idx > b.idx:
                    best[d.eng] = d
        out.extend(best.values())
        for d in out:
            d.signal = True
        o.deps = out
        o.idx = len(self.streams[eng])
        self.streams[eng].append(o)
        if touch_d:
            if o.is_dma:
                if read_d:
                    self.cur_dma.append(o)
            else:
                self.cur_last[eng] = o
        if final:
            o.signal = True
            self.final_dmas.append(o)
        return o

    def emit(self, nc, es):
        eng_sem = {}
        for e in ("pe", "act", "dve", "pool"):
            eng_sem[e] = es.enter_context(nc.semaphore("s_" + e))
        dma_sem = {}
        dma_cnt = {}
        for e in ENGS:
            cnt = 0
            for o in self.streams[e]:
                if o.is_dma:
                    if o.semkey not in dma_sem:
                        nm = "d_" + "_".join(str(x) for x in (o.semkey if isinstance(o.semkey, tuple) else (o.semkey,)))
                        dma_sem[o.semkey] = es.enter_context(nc.semaphore(nm))
                        dma_cnt[o.semkey] = 0
                    dma_cnt[o.semkey] += 16
                    o.tick = dma_cnt[o.semkey]
                elif o.signal:
                    assert e != "sp"
                    cnt += 1
                    o.tick = cnt
        block = es.enter_context(nc.Block())
        engobj = {"pe": block.tensor, "act": block.scalar, "dve": block.vector,
                  "pool": block.gpsimd, "sp": block.sync}

        def run(e):
            def body(eng):
                waited = {}
                for o in self.streams[e]:
                    need = {}
                    for d in o.deps:
                        sem = dma_sem[d.semkey] if d.is_dma else eng_sem[d.eng]
                        k = id(sem)
                        if k not in need or need[k][1] < d.tick:
                            need[k] = (sem, d.tick)
                    for k, (sem, tick) in need.items():
                        if waited.get(k, 0) >= tick:
                            continue
                        eng.wait_ge(sem, tick)
                        waited[k] = tick
                    ins = o.fn(eng)
                    if o.is_dma:
                        ins.then_inc(dma_sem[o.semkey], 16)
                    elif o.signal:
                        ins.then_inc(eng_sem[e], 1)
                if e == "sp":
                    last = {}
                    for o in self.final_dmas:
                        last[o.semkey] = max(last.get(o.semkey, 0), o.tick)
                    for key, tick in last.items():
                        sem = dma_sem[key]
                        if waited.get(id(sem), 0) >= tick:
                            continue
                        eng.wait_ge(sem, tick)
                        waited[id(sem)] = tick
            engobj[e](body)

        for e in ENGS:
            run(e)


def build_nc(upto="P", debug=False):
    nc = bass.Bass("TRN2", target_bir_lowering=False)
    DBG = []

    def din(name, shape):
        return nc.dram_tensor(name, list(shape), F32, kind="ExternalInput").ap()

    def dout(name, shape):
        return nc.dram_tensor(name, list(shape), F32, kind="ExternalOutput").ap()

    x_d = din("x", (TK, DM))
    p_d = din("p", (TK, 256))
    ck_d = din("ck", (16, 128, 128))
    cv_d = din("cv", (16, 128, 128))
    w1g_d = din("w1g", (DM, DFF)); w1u_d = din("w1u", (DM, DFF)); w1d_d = din("w1d", (DFF, DM))
    w2g_d = din("w2g", (DM, DFF)); w2u_d = din("w2u", (DM, DFF)); w2d_d = din("w2d", (DFF, DM))
    win_d = din("win", (DM, 1792)); wo_d = din("wo", (DM, DM))
    wpg_d = din("wpg", (DM, DM)); wpp_d = din("wpp", (256, DM))
    g1_d = din("g1", (1, DM)); gm_d = din("gm", (1, DM)); g2_d = din("g2", (1, DM)); gp_d = din("gp", (1, DM))
    gav_d = din("gav", (1, 512)); bav_d = din("bav", (1, 512)); goa_d = din("goa", (1, 512)); gob_d = din("gob", (1, 512))
    gq_d = din("gq", (1, 64)); gk_d = din("gk", (1, 64)); sinks_d = din("sinks", (1, 8))
    ws_d = din("ws", (8, 128, 128)); bs_d = din("bs", (8, 128))
    ident_d = din("ident", (128, 128)); pos_d = din("pos", (128, NT)); invf_d = din("invf", (128, 32))
    mcur_d = din("mcur", (128, 128)); mprev_d = din("mprev", (128, 128))
    mc_d = din("mc", (128, 8)); mnew_d = din("mnew", (128, 128)); sel_d = din("sel", (8, 128))

    y_d = dout("y", (TK, DM))
    pk_d = dout("pk", (128, 128)); pv_d = dout("pv", (128, 128)); pa_d = dout("pa", (128, 512))
    sk_d = dout("sk", (16, 128, 128)); sv_d = dout("sv", (16, 128, 128)); sa_d = dout("sa", (128, 512))

    T = Tracker()
    R = T.r

    def RD(name, *idx):
        return T.r(name, *idx, d=True)

    with ExitStack() as es:
        def sb(name, shape, dt=F32):
            return es.enter_context(nc.sbuf_tensor("sb_" + name, list(shape), dt))

        X = sb("X", (128, NT, DM))
        identf = sb("identf", (128, 128))
        identb = sb("identb", (128, 128), BF16)
        G = [sb("G0", (128, DM)), sb("G1", (128, DM))]
        CS = sb("CS", (128, 2, NT, 32))
        mcur = sb("mcur", (128, 128), BF16); mprev = sb("mprev", (128, 128), BF16)
        mnew = sb("mnew", (128, 128), BF16); mc = sb("mc", (128, 8), BF16)
        ssq = sb("ssq", (128, NT)); rstd = sb("rstd", (128, NT)); neghalf = sb("neghalf", (128, NT))
        st = sb("st", (128, 64))
        bnst = sb("bnst", (128, 6))
        esink = sb("esink", (128, 8))
        posf = sb("posf", (128, NT)); invf = sb("invf", (128, 32))
        dbytes = (nc.sbuf_bytes_remaining - 1024) // 4 * 4
        assert dbytes >= 126000, dbytes
        Dreg = sb("Dreg", (128, dbytes // 4))
        ps = [es.enter_context(nc.psum_tensor(f"ps{i}", [128, 512], F32)) for i in range(8)]
        psb = [p[:, :].bitcast(BF16) for p in ps]

        class Lay:
            def __init__(self):
                self.off = 0

            def take(self, nbytes, dt=F32):
                assert self.off % 4 == 0
                a = self.off // 4
                n = (nbytes + 3) // 4
                self.off += n * 4
                assert self.off <= dbytes, (self.off, dbytes)
                v = Dreg[:, a:a + n]
                if dt != F32:
                    v = v.bitcast(dt)
                return v

        T.op("sp", lambda e: e.dma_start(out=identf[:], in_=ident_d), writes=[R("identf")], dma="c_ident")
        T.op("sp", lambda e: e.dma_start(out=posf[:], in_=pos_d), writes=[R("posf")], dma="c_pos")
        T.op("sp", lambda e: e.dma_start(out=invf[:], in_=invf_d), writes=[R("invf")], dma="c_invf")
        for nm, dst, src in (("mcur", mcur, mcur_d), ("mprev", mprev, mprev_d), ("mnew", mnew, mnew_d), ("mc", mc, mc_d)):
            T.op("pool", lambda e, dst=dst, src=src: e.dma_start(out=dst[:], in_=src), writes=[R(nm)], dma="c_" + nm)
        T.op("dve", lambda e: e.tensor_copy(out=identb[:], in_=identf[:]), reads=[R("identf")], writes=[R("identb")])
        T.op("dve", lambda e: e.memset(neghalf[:], -0.5), writes=[R("neghalf")])
        def rope_tables():
            lay = Lay()
            lay.off = 110 * 1024
            NA = 2 * NT * 32
            ANG = lay.take(NA * 4)
            KI = lay.take(NA * 4, I32)
            KF = lay.take(NA * 4)
            ang3 = ANG.rearrange("p (c t j) -> p c t j", c=2, t=NT)
            C1 = 6.28125
            C2 = 2.0 * math.pi - 6.28125
            NH = NT * 32
            SA = ANG[:, NH:2 * NH]
            CA = ANG[:, 0:NH]
            KIh = KI[:, 0:NH]
            KFh = KF[:, 0:NH]
            T.op("dve", lambda e: e.tensor_tensor(out=ang3[:, 1], in0=posf[:, :].unsqueeze(2).to_broadcast([128, NT, 32]),
                                                  in1=invf[:, :].unsqueeze(1).to_broadcast([128, NT, 32]), op=ALU.mult),
                 reads=[R("posf"), R("invf")], writes=[R("ANG")])
            T.op("dve", lambda e: e.tensor_scalar(out=KIh, in0=SA, scalar1=1.0 / (2 * math.pi), scalar2=None, op0=ALU.mult),
                 reads=[R("ANG")], writes=[R("KI")])
            T.op("dve", lambda e: e.tensor_copy(out=KFh, in_=KIh), reads=[R("KI")], writes=[R("KF")])
            T.op("dve", lambda e: e.scalar_tensor_tensor(out=SA, in0=KFh, scalar=-C1, in1=SA, op0=ALU.mult, op1=ALU.add),
                 reads=[R("KF"), R("ANG")], writes=[R("ANG")])
            T.op("dve", lambda e: e.scalar_tensor_tensor(out=SA, in0=KFh, scalar=-C2, in1=SA, op0=ALU.mult, op1=ALU.add),
                 reads=[R("KF"), R("ANG")], writes=[R("ANG")])
            T.op("dve", lambda e: e.tensor_scalar(out=CA, in0=SA, scalar1=math.pi / 2, scalar2=None, op0=ALU.add),
                 reads=[R("ANG")], writes=[R("ANG")])
            T.op("dve", lambda e: e.tensor_scalar(out=KF, in0=ANG, scalar1=math.pi, scalar2=-2 * math.pi, op0=ALU.is_gt, op1=ALU.mult),
                 reads=[R("ANG")], writes=[R("KF")])
            T.op("dve", lambda e: e.tensor_tensor(out=ANG, in0=ANG, in1=KF, op=ALU.add), reads=[R("ANG"), R("KF")], writes=[R("ANG")])
            T.op("dve", lambda e: e.tensor_scalar(out=KF, in0=ANG, scalar1=math.pi, scalar2=-2 * math.pi, op0=ALU.is_gt, op1=ALU.mult),
                 reads=[R("ANG")], writes=[R("KF")])
            T.op("dve", lambda e: e.tensor_tensor(out=ANG, in0=ANG, in1=KF, op=ALU.add), reads=[R("ANG"), R("KF")], writes=[R("ANG")])
            T.op("dve", lambda e: e.tensor_scalar(out=KF, in0=ANG, scalar1=-math.pi, scalar2=2 * math.pi, op0=ALU.is_lt, op1=ALU.mult),
                 reads=[R("ANG")], writes=[R("KF")])
            T.op("dve", lambda e: e.tensor_tensor(out=ANG, in0=ANG, in1=KF, op=ALU.add), reads=[R("ANG"), R("KF")], writes=[R("ANG")])
            T.op("dve", lambda e: e.tensor_scalar(out=ANG, in0=ANG, scalar1=-3.14159, scalar2=3.14159, op0=ALU.max, op1=ALU.min),
                 reads=[R("ANG")], writes=[R("ANG")])
            T.op("act", lambda e: e.activation(out=CS[:].rearrange("p c t j -> p (c t j)"), in_=ANG, func=AF.Sin),
                 reads=[R("ANG")], writes=[R("CS")])

            T.op("sp", lambda e: e.dma_start(out=sk_d[:, 0:120, :], in_=ck_d[:, 8:128, :]), dma="o_sk0", final=True)
            T.op("sp", lambda e: e.dma_start(out=sv_d[:, 0:120, :], in_=cv_d[:, 8:128, :]), dma="o_sv0", final=True)


        gcount = [0]

        def load_gain(g_dram):
            gs = gcount[0] % 2
            gcount[0] += 1
            T.op("sp", lambda e: e.dma_start(out=G[gs][:], in_=g_dram.partition_broadcast(128)),
                 writes=[R("G", gs)], dma=("G", gs))
            return gs

        def rstd_ops(src, dst, nh, scale, eps, rkeys_in, rkeys_out):
            T.op("dve", lambda e: e.tensor_scalar(out=dst, in0=src, scalar1=scale, scalar2=eps, op0=ALU.mult, op1=ALU.add),
                 reads=rkeys_in, writes=rkeys_out)
            T.op("pool", lambda e: e.tensor_tensor(out=dst, in0=dst, in1=nh, op=ALU.pow),
                 reads=rkeys_out + [R("neghalf")], writes=rkeys_out)

        layz = Lay()
        layz.off = ZOFF
        WIN = layz.take(8 * 1792 * 2, BF16).rearrange("p (k c) -> p k c", k=8)
        layz = Lay()
        layz.off = ZOFF
        WPG = layz.take(8 * DM * 2, BF16).rearrange("p (k c) -> p k c", k=8)
        WPP = layz.take(2 * DM * 2, BF16).rearrange("p (k c) -> p k c", k=2)
        WINR = [R("WIN", i) for i in range(10)]

        def prefetch_M():
            win3 = win_d.rearrange("(k p) c -> p k c", p=128)
            T.op("pool", lambda e: e.dma_start(out=WIN[:, :, 0:1024], in_=win3[:, :, 0:1024]), reads=[R("CS")], writes=[R("WIN", 0)], dma="w_in0")
            for kh in range(2):
                for b in range(4):
                    T.op("pool", lambda e, kh=kh, b=b: e.dma_start(
                        out=WIN[:, :, 1024 + b * 128 + kh * 64:1024 + b * 128 + kh * 64 + 64],
                        in_=win3[:, :, 1024 + (kh * 4 + b) * 64:1024 + (kh * 4 + b) * 64 + 64]),
                        reads=[R("CS")], writes=[R("WIN", 1 + kh * 4 + b)], dma=("w_in1", kh * 4 + b))
            T.op("pool", lambda e: e.dma_start(out=WIN[:, :, 1536:1792], in_=win3[:, :, 1536:1792]), reads=[R("CS")], writes=[R("WIN", 9)], dma="w_in3")

        def prefetch_P():
            T.op("pool", lambda e: e.dma_start(out=WPG[:], in_=wpg_d.rearrange("(k p) c -> p k c", p=128)), writes=[R("WPG")] + WINR, dma="w_pg")
            T.op("pool", lambda e: e.dma_start(out=WPP[:], in_=wpp_d.rearrange("(k p) c -> p k c", p=128)), writes=[R("WPP")] + WINR, dma="w_pp")

        def ffn_phase(tag, g_dram, Wg, Wu, Wd, have_stats=False, after_prologue=None, first=False):
            T.new_phase()
            lay = Lay()
            XNT = lay.take(8 * TK * 2, BF16).rearrange("p (k t) -> p k t", k=8)
            PMAX = max(b - a for a, b in PIECES)
            HT = lay.take(PMAX * TK * 2, BF16).rearrange("p (j t) -> p j t", j=PMAX)
            WGs = [lay.take(8 * 128 * 2, BF16).rearrange("p (k c) -> p k c", k=8) for _ in range(NS_GU)]
            WUs = [lay.take(8 * 128 * 2, BF16).rearrange("p (k c) -> p k c", k=8) for _ in range(NS_GU)]
            WDs = [lay.take(DM * 2, BF16) for _ in range(NS_D)]
            SG = [lay.take(512 * 4) for _ in range(2)]
            HNB = [lay.take(DM * 2, BF16) for _ in range(2)]
            JUNK = lay.take(DM * 2, BF16)
            assert lay.off <= ZOFF, lay.off
            gs = load_gain(g_dram)
            if first:
                for t in range(NT):
                    T.op("sp", lambda e, t=t: e.dma_start(out=X[:, t, :], in_=x_d[t * 128:(t + 1) * 128, :]),
                         writes=[R("X", t)], dma=("x", t))
            Wg3 = Wg.rearrange("(k p) c -> p k c", p=128)
            Wu3 = Wu.rearrange("(k p) c -> p k c", p=128)

            def load_gu(j):
                s = j % NS_GU
                T.op("pool", lambda e: e.dma_start(out=WGs[s][:], in_=Wg3[:, :, j * 128:(j + 1) * 128]),
                     writes=[RD("WG", s)], dma=("wg", s))
                T.op("pool", lambda e: e.dma_start(out=WUs[s][:], in_=Wu3[:, :, j * 128:(j + 1) * 128]),
                     writes=[RD("WU", s)], dma=("wu", s))

            def load_d(j):
                s = j % NS_D
                T.op("pool", lambda e: e.dma_start(out=WDs[s], in_=Wd[j * 128:(j + 1) * 128, :]),
                     writes=[RD("WD", s)], dma=("wd", s))

            load_gu(0)
            load_gu(1)
            pending_loads = [lambda: load_gu(2)]

            acnt = [0]

            def emitA(j, gi):
                jj = j - [p0 for (p0, p1) in PIECES if p0 <= j < p1][0]
                s = j % NS_GU
                c0, n = GROUPS[gi]
                tiles = list(range(c0 // 128, (c0 + n) // 128))
                pair = acnt[0] % 2
                acnt[0] += 1
                bg, bu = 2 * pair, 2 * pair + 1

                def mmA(e):
                    for k in range(8):
                        e.matmul(ps[bg][:, 0:n], lhsT=WGs[s][:, k, :], rhs=XNT[:, k, c0:c0 + n], start=(k == 0), stop=(k == 7))
                    for k in range(8):
                        ins = e.matmul(ps[bu][:, 0:n], lhsT=WUs[s][:, k, :], rhs=XNT[:, k, c0:c0 + n], start=(k == 0), stop=(k == 7))
                    return ins
                T.op("pe", mmA, reads=[RD("WG", s), RD("WU", s)] + [RD("XNT", t) for t in tiles],
                     writes=[R("ps", bg), R("ps", bu)])
                T.op("act", lambda e: e.activation(out=SG[pair][:, 0:n], in_=ps[bg][:, 0:n], func=AF.Silu),
                     reads=[R("ps", bg)], writes=[RD("SG", pair)])
                T.op("dve", lambda e: e.tensor_tensor(out=HT[:, jj, c0:c0 + n], in0=SG[pair][:, 0:n], in1=ps[bu][:, 0:n], op=ALU.mult),
                     reads=[RD("SG", pair), R("ps", bu)], writes=[RD("HT", jj, gi)])

            def sq_group(gi):
                c0, n = GROUPS[gi]
                for t in range(c0 // 128, (c0 + n) // 128):
                    T.op("act", lambda e, t=t: e.activation(out=JUNK, in_=X[:, t, :], func=AF.Square, accum_out=ssq[:, t:t + 1]),
                         reads=[R("X", t)], writes=[RD("JUNK"), R("ssq", t)])

            if not have_stats:
                sq_group(0)
                sq_group(1)
            for gi, (c0, n) in enumerate(GROUPS):
                tiles = list(range(c0 // 128, (c0 + n) // 128))
                t0, t1 = tiles[0], tiles[-1] + 1
                if not have_stats:
                    rstd_ops(ssq[:, t0:t1], rstd[:, t0:t1], neghalf[:, t0:t1], 1.0 / DM, EPS,
                             [R("ssq", t) for t in tiles], [R("rstd", t) for t in tiles])
                for t in tiles:
                    hb = HNB[t % 2]
                    T.op("dve", lambda e, t=t, hb=hb: e.scalar_tensor_tensor(out=hb, in0=X[:, t, :], scalar=rstd[:, t:t + 1], in1=G[gs][:],
                                                                             op0=ALU.mult, op1=ALU.mult),
                         reads=[R("X", t), R("rstd", t), R("G", gs)], writes=[RD("HNB", t % 2)])
                    bank = 4 + t % 4

                    def tr(e, hb=hb, bank=bank):
                        for k in range(8):
                            ins = e.transpose(out=psb[bank][:, k * 128:(k + 1) * 128], in_=hb[:, k * 128:(k + 1) * 128], identity=identb[:])
                        return ins
                    T.op("pe", tr, reads=[RD("HNB", t % 2), R("identb")], writes=[R("ps", bank)])
                    if t % 2 == 0 or have_stats:
                        T.op("act", lambda e, t=t, bank=bank: e.activation(out=XNT[:, :, t * 128:(t + 1) * 128],
                                                                          in_=psb[bank][:, :].rearrange("p (k c) -> p k c", k=8), func=AF.Copy),
                             reads=[R("ps", bank)], writes=[RD("XNT", t)])
                    else:
                        T.op("dve", lambda e, t=t, bank=bank: e.tensor_copy(out=XNT[:, :, t * 128:(t + 1) * 128],
                                                                           in_=psb[bank][:, :].rearrange("p (k c) -> p k c", k=8)),
                             reads=[R("ps", bank)], writes=[RD("XNT", t)])
                if gi + 2 < len(GROUPS) and not have_stats:
                    sq_group(gi + 2)
                if pending_loads:
                    pending_loads.pop(0)()
                if gi >= 1:
                    emitA(0, gi - 1)
            emitA(0, len(GROUPS) - 1)

            if first:
                rope_tables()
            for j in range(NS_D):
                load_d(j)
            bcnt = 0
            for (j0, j1) in PIECES:
                for j in range(j0, j1):
                    if j > 0:
                        for gi in range(len(GROUPS)):
                            emitA(j, gi)
                    if j + NS_GU < NCH:
                        load_gu(j + NS_GU)
                npj = j1 - j0
                for t in range(NT):
                    gi = min(t // 4, 4)
                    for hf in range(2):
                        bank = 4 + bcnt % 4
                        bcnt += 1

                        def mmB(e, t=t, hf=hf, bank=bank, j0=j0, npj=npj):
                            for jj in range(npj):
                                ins = e.matmul(ps[bank][:, :], lhsT=HT[:, jj, t * 128:(t + 1) * 128],
                                               rhs=WDs[(j0 + jj) % NS_D][:, hf * 512:(hf + 1) * 512], start=(jj == 0), stop=(jj == npj - 1))
                            return ins
                        T.op("pe", mmB, reads=[RD("HT", jj, gi) for jj in range(npj)] + [RD("WD", (j0 + jj) % NS_D) for jj in range(npj)],
                             writes=[R("ps", bank)])
                        T.op("dve", lambda e, t=t, hf=hf, bank=bank: e.scalar_tensor_tensor(out=X[:, t, hf * 512:(hf + 1) * 512], in0=ps[bank][:, :], scalar=0.5,
                                                                                          in1=X[:, t, hf * 512:(hf + 1) * 512], op0=ALU.mult, op1=ALU.add),
                             reads=[R("ps", bank), R("X", t)], writes=[R("X", t)])
                    if j1 == NCH:
                        T.op("act", lambda e, t=t: e.activation(out=JUNK, in_=X[:, t, :], func=AF.Square, accum_out=ssq[:, t:t + 1]),
                             reads=[R("X", t)], writes=[RD("JUNK"), R("ssq", t)])
                if j1 == NCH:
                    rstd_ops(ssq[:, :], rstd[:, :], neghalf[:, :], 1.0 / DM, EPS, [R("ssq", t) for t in range(NT)], [R("rstd", t) for t in range(NT)])
                for j in range(j0 + NS_D, min(j1 + NS_D, NCH)):
                    load_d(j)
                if j0 == 0 and after_prologue is not None:
                    after_prologue()

        def mix_phase(ntiles=NT, laststep=99):
            T.new_phase()
            lay = Lay()
            WO = lay.take(8 * DM * 2, BF16).rearrange("p (k c) -> p k c", k=8)
            GAV = lay.take(2048); BAV = lay.take(2048); GOA = lay.take(2048); GOB = lay.take(2048)
            GQK = lay.take(640 * 4)
            GQ1 = lay.take(64 * 4); GK1 = lay.take(64 * 4)
            WST = lay.take(8 * 128 * 2, BF16).rearrange("p (h t) -> p h t", h=8)
            WBD = lay.take(8 * 128 * 2, BF16).rearrange("p (h t) -> p h t", h=8)
            BST = lay.take(8 * 4); BSS = lay.take(8 * 4)
            SELB = lay.take(128 * 2, BF16)
            KT = lay.take(NT * 128 * 2, BF16).rearrange("p (t s) -> p t s", t=NT)
            VAUG = lay.take(NT * 130 * 2, BF16).rearrange("p (t k d) -> p t k d", t=NT, k=2)
            KCT = lay.take(16 * 128 * 2, BF16).rearrange("p (n s) -> p n s", n=16)
            VC = lay.take(16 * 130 * 2, BF16).rearrange("p (n k d) -> p n k d", n=16, k=2)
            KCraw = lay.take(16 * 128 * 2)
            KC = KCraw.bitcast(BF16).rearrange("p (n c) -> p n c", n=16)
            HN = lay.take(DM * 2, BF16)
            HNT = lay.take(DM * 2, BF16)
            JUNKM = lay.take(DM * 2, BF16)
            PTraw = lay.take(DM * 4, BF16)
            PT = PTraw.rearrange("p (i c) -> p i c", i=4)
            PTC = PTraw[:, 1024:2048]
            GEL = lay.take(DM * 4)
            WSF = GEL.rearrange("p (h s) -> p h s", h=8)
            VAB = lay.take(512 * 2, BF16)
            SQQK = lay.take(640 * 4)
            QKH = lay.take(640 * 4)
            RT = lay.take(640 * 4)
            QR = lay.take(640 * 2, BF16)
            QTs = [lay.take(512 * 2, BF16).rearrange("p (b t) -> p b t", b=4) for _ in range(2)]
            OB = lay.take(2048)
            A2b = lay.take(2048)
            TMP = lay.take(2048)
            MIXs = [lay.take(DM * 2, BF16) for _ in range(2)]
            MIXT = lay.take(DM * 2, BF16)
            KFo = lay.take(512); VFo = lay.take(512); VAF = lay.take(2048)
            OTS = KCraw.rearrange("p (k c) -> p k c", k=2)

            assert lay.off <= ZOFF, lay.off
            if debug:
                for nm, ap_, res_ in (("HN", HN, RD("HN")), ("OB", OB, RD("OB")), ("KT", KT.rearrange("p t s -> p (t s)"), RD("KT", 0))):
                    DBG.append((nm, ap_, res_))
            gs = load_gain(gm_d)
            T.op("pool", lambda e: e.dma_start(out=WO[:], in_=wo_d.rearrange("(k p) c -> p k c", p=128)), writes=[RD("WO")], dma="w_o")
            T.op("pool", lambda e: e.dma_start(out=KC[:], in_=ck_d.rearrange("n s c -> s n c")), writes=[RD("KC")], dma="c_kc")
            for kh in range(2):
                T.op("pool", lambda e, kh=kh: e.dma_start(out=VC[:, :, kh, 0:64], in_=cv_d[:, :, kh * 64:(kh + 1) * 64].rearrange("n s d -> s n d")),
                     writes=[RD("VC", kh)], dma=("c_vc", kh))
            for nm, dst, src in (("GAV", GAV, gav_d), ("BAV", BAV, bav_d), ("GOA", GOA, goa_d), ("GOB", GOB, gob_d),
                                 ("GQ1", GQ1, gq_d), ("GK1", GK1, gk_d)):
                T.op("sp", lambda e, dst=dst, src=src: e.dma_start(out=dst, in_=src.partition_broadcast(128)), writes=[RD(nm)], dma="c_" + nm)
            T.op("sp", lambda e: e.dma_start(out=esink[:], in_=sinks_d.partition_broadcast(128)), writes=[R("esink")], dma="c_sink")
            T.op("sp", lambda e: e.dma_start(out=WSF, in_=ws_d.rearrange("h t s -> t h s")), writes=[RD("GEL", 0), RD("GEL", 1)], dma="c_wsf")
            T.op("sp", lambda e: e.dma_start(out=BST, in_=bs_d.rearrange("h t -> t h"), allow_slow_non_contiguous=True), writes=[RD("BST")], dma="c_bst")
            T.op("pool", lambda e: e.dma_start(out=SELB[0:8, :], in_=sel_d), writes=[RD("SELB")], dma="c_sel")
            for n in range(16):
                T.op("sp", lambda e, n=n: e.dma_start(out=BSS[8 * n:8 * n + 8, :], in_=bs_d[:, 0:8].rearrange("h i -> i h"), allow_slow_non_contiguous=True),
                     writes=[RD("BSS", n)], dma=("c_bss", n % 2))
            BSSR = [RD("BSS", n) for n in range(16)]
            T.op("act", lambda e: e.activation(out=esink[:], in_=esink[:], func=AF.Exp), reads=[R("esink")], writes=[R("esink")])
            T.op("dve", lambda e: e.tensor_copy(out=GQK[:, 0:512].rearrange("p (h d) -> p h d", h=8), in_=GQ1.unsqueeze(1).to_broadcast([128, 8, 64])),
                 reads=[RD("GQ1")], writes=[RD("GQK")])
            T.op("dve", lambda e: e.tensor_copy(out=GQK[:, 512:640].rearrange("p (h d) -> p h d", h=2), in_=GK1.unsqueeze(1).to_broadcast([128, 2, 64])),
                 reads=[RD("GK1")], writes=[RD("GQK")])
            T.op("dve", lambda e: e.memset(VAUG[:, :, :, 64:65], 1.0), writes=[RD("VAUG1")])
            T.op("dve", lambda e: e.memset(VC[:, :, :, 64:65], 1.0), writes=[RD("VC1")])
            for half in range(2):
                def trw(e, half=half):
                    for hh in range(4):
                        ins = e.transpose(out=ps[5 + half][:, hh * 128:(hh + 1) * 128], in_=WSF[:, half * 4 + hh, :], identity=identf[:])
                    return ins
                T.op("pe", trw, reads=[RD("GEL", 0), RD("GEL", 1), R("identf")], writes=[R("ps", 5 + half)])
                T.op("dve", lambda e, half=half: e.tensor_tensor(out=WST[:, half * 4:half * 4 + 4, :], in0=ps[5 + half][:, :].rearrange("p (h t) -> p h t", h=4),
                                                                 in1=mcur[:, :].unsqueeze(1).to_broadcast([128, 4, 128]), op=ALU.mult),
                     reads=[R("ps", 5 + half), R("mcur")], writes=[RD("WST")])
            T.op("pe", lambda e: e.matmul(ps[7][:, 0:64], lhsT=SELB[0:8, :], rhs=WST[0:8, :, 0:8], start=True, stop=True),
                 reads=[RD("SELB"), RD("WST")], writes=[R("ps", 7)])
            T.op("dve", lambda e: e.tensor_tensor(out=WBD[:, :, :].rearrange("p h (n i) -> p h n i", n=16),
                                                  in0=ps[7][:, 0:64].rearrange("p (h i) -> p h i", h=8).unsqueeze(2).to_broadcast([128, 8, 16, 8]),
                                                  in1=mnew[:, :].rearrange("p (n i) -> p n i", n=16).unsqueeze(1).to_broadcast([128, 8, 16, 8]), op=ALU.mult),
                 reads=[R("ps", 7), R("mnew")], writes=[RD("WBD")])
            for half in range(2):
                def trk(e, half=half):
                    for nn in range(8):
                        ins = e.transpose(out=psb[3 + half][:, nn * 128:(nn + 1) * 128], in_=KC[:, half * 8 + nn, :], identity=identb[:])
                    return ins
                T.op("pe", trk, reads=[RD("KC"), R("identb")], writes=[R("ps", 3 + half)])
                T.op("act", lambda e, half=half: e.activation(out=KCT[:, half * 8:half * 8 + 8, :], in_=psb[3 + half][:, :].rearrange("p (n s) -> p n s", n=8), func=AF.Copy),
                     reads=[R("ps", 3 + half)], writes=[RD("KCT")])

            HNT3 = HNT.rearrange("p (k t) -> p k t", k=8)
            MIXT3 = MIXT.rearrange("p (k t) -> p k t", k=8)
            q3 = QKH.rearrange("p (h d) -> p h d", h=10)
            x1, x2 = q3[:, :, 0:32], q3[:, :, 32:64]
            t1 = SQQK[:, 0:320].rearrange("p (h d) -> p h d", h=10)
            t3 = SQQK[:, 320:640].rearrange("p (h d) -> p h d", h=10)
            t2 = RT[:, 0:320].rearrange("p (h d) -> p h d", h=10)
            t4 = RT[:, 320:640].rearrange("p (h d) -> p h d", h=10)
            qr3 = QR.rearrange("p (h d) -> p h d", h=10)
            VP = GEL[:, 512:1024]

            def A1(t):
                T.op("dve", lambda e: e.scalar_tensor_tensor(out=HN, in0=X[:, t, :], scalar=rstd[:, t:t + 1], in1=G[gs][:], op0=ALU.mult, op1=ALU.mult),
                     reads=[R("X", t), R("rstd", t), R("G", gs)], writes=[RD("HN")])

            def ZA(t):
                def tr1(e):
                    for k in range(8):
                        ins = e.transpose(out=psb[0][:, k * 128:(k + 1) * 128], in_=HN[:, k * 128:(k + 1) * 128], identity=identb[:])
                    return ins
                T.op("pe", tr1, reads=[RD("HN"), R("identb")], writes=[R("ps", 0)])
                T.op("act", lambda e: e.activation(out=HNT, in_=psb[0][:, :], func=AF.Copy), reads=[R("ps", 0)], writes=[RD("HNT")])

            def ZB(t, part=2):
                def mmz(e, groups):
                    for c, a, b in groups:
                        for k in range(8):
                            ins = e.matmul(ps[1 + c][:, 0:b - a], lhsT=HNT3[:, k, :], rhs=WIN[:, k, a:b], start=(k == 0), stop=(k == 7))
                    return ins
                if part in (0, 2):
                    T.op("pe", lambda e: mmz(e, ((2, 1024, 1536), (3, 1536, 1792))), reads=[RD("HNT")] + WINR, writes=[R("ps", 3), R("ps", 4), R("ps4d")])
                if part in (1, 2):
                    T.op("pe", lambda e: mmz(e, ((0, 0, 512), (1, 512, 1024))), reads=[RD("HNT")] + WINR, writes=[R("ps", 1), R("ps", 2)])

            def B1g(t):
                for c in range(2):
                    T.op("act", lambda e, c=c: e.activation(out=GEL[:, c * 512:(c + 1) * 512], in_=ps[1 + c][:, :], func=AF.Gelu_apprx_tanh),
                         reads=[R("ps", 1 + c)], writes=[RD("GEL", c)])

            def B1a(t):
                S = (t == NT - 1)
                outt = t >= NT - 2
                T.op("act", lambda e: e.activation(out=SQQK[:, 0:512], in_=ps[3][:, :], func=AF.Square), reads=[R("ps", 3)], writes=[RD("SQQK"), RD("SQQK2")])
                T.op("act", lambda e: e.activation(out=SQQK[:, 512:640], in_=ps[4][:, 0:128], func=AF.Square), reads=[R("ps", 4)], writes=[RD("SQQK2")])
                T.op("dve", lambda e: e.tensor_reduce(out=st[:, 8:18], in_=SQQK.rearrange("p (h d) -> p h d", h=10), axis=AX.X, op=ALU.add),
                     reads=[RD("SQQK"), RD("SQQK2")], writes=[R("st_sh")])
                rstd_ops(st[:, 8:18], st[:, 20:30], neghalf[:, 0:10], 1.0 / 64, EPS, [R("st_sh")], [R("st_rh")])
                T.op("act", lambda e: e.activation(out=VAUG[:, t, :, 0:64], in_=ps[4][:, 128:256].rearrange("p (k d) -> p k d", k=2), func=AF.Copy),
                     reads=[R("ps", 4)], writes=[RD("VAUG", t)])
                if outt:
                    T.op("act", lambda e: e.activation(out=VFo, in_=ps[4][:, 128:256], func=AF.Copy), reads=[R("ps", 4)], writes=[RD("VFo")])
                    if S:
                        T.op("sp", lambda e: e.dma_start(out=sv_d[:, 120:128, :], in_=VFo), reads=[RD("VFo")], dma="o_sv1", final=True)
                    else:
                        T.op("sp", lambda e: e.dma_start(out=pv_d, in_=VFo), reads=[RD("VFo")], dma="o_pv", final=True)

            def B1a2(t):
                T.op("dve", lambda e: e.tensor_tensor(out=QKH[:, 0:512].rearrange("p (h d) -> p h d", h=8), in0=ps[3][:, :].rearrange("p (h d) -> p h d", h=8),
                                                      in1=st[:, 20:28].unsqueeze(2).to_broadcast([128, 8, 64]), op=ALU.mult),
                     reads=[R("ps", 3), R("st_rh")], writes=[RD("QKH")])
                T.op("dve", lambda e: e.tensor_tensor(out=QKH[:, 512:640].rearrange("p (h d) -> p h d", h=2), in0=ps[4][:, 0:128].rearrange("p (h d) -> p h d", h=2),
                                                      in1=st[:, 28:30].unsqueeze(2).to_broadcast([128, 2, 64]), op=ALU.mult),
                     reads=[R("ps", 4), R("st_rh")], writes=[RD("QKH")])

            def B1b(t):
                S = (t == NT - 1)
                outt = t >= NT - 2
                T.op("dve", lambda e: e.tensor_tensor(out=QKH, in0=QKH, in1=GQK, op=ALU.mult), reads=[RD("QKH"), RD("GQK")], writes=[RD("QKH")])
                cosb = CS[:, 0, t, :].unsqueeze(1).to_broadcast([128, 10, 32])
                sinb = CS[:, 1, t, :].unsqueeze(1).to_broadcast([128, 10, 32])
                T.op("pool", lambda e: e.tensor_tensor(out=t2, in0=x2, in1=sinb, op=ALU.mult), reads=[RD("QKH"), R("CS")], writes=[RD("RT")])
                T.op("pool", lambda e: e.tensor_tensor(out=t4, in0=x1, in1=sinb, op=ALU.mult), reads=[RD("QKH"), R("CS")], writes=[RD("RT2")])
                T.op("dve", lambda e: e.tensor_tensor(out=t1, in0=x1, in1=cosb, op=ALU.mult), reads=[RD("QKH"), R("CS")], writes=[RD("SQQK")])
                T.op("dve", lambda e: e.tensor_tensor(out=t3, in0=x2, in1=cosb, op=ALU.mult), reads=[RD("QKH"), R("CS")], writes=[RD("SQQK2")])
                T.op("dve", lambda e: e.tensor_tensor(out=qr3[:, :, 0:32], in0=t1, in1=t2, op=ALU.subtract), reads=[RD("SQQK"), RD("RT")], writes=[RD("QR")])
                T.op("dve", lambda e: e.tensor_tensor(out=qr3[:, :, 32:64], in0=t3, in1=t4, op=ALU.add), reads=[RD("SQQK2"), RD("RT2")], writes=[RD("QR")])
                if outt:
                    kf3 = KFo.rearrange("p (h d) -> p h d", h=2)
                    T.op("pool", lambda e: e.tensor_tensor(out=kf3[:, :, 0:32], in0=t1[:, 8:10, :], in1=t2[:, 8:10, :], op=ALU.subtract),
                         reads=[RD("SQQK"), RD("RT")], writes=[RD("KFo")])
                    T.op("pool", lambda e: e.tensor_tensor(out=kf3[:, :, 32:64], in0=t3[:, 8:10, :], in1=t4[:, 8:10, :], op=ALU.add),
                         reads=[RD("SQQK2"), RD("RT2")], writes=[RD("KFo")])
                    if S:
                        T.op("sp", lambda e: e.dma_start(out=sk_d[:, 120:128, :], in_=KFo), reads=[RD("KFo")], dma="o_sk1", final=True)
                    else:
                        T.op("sp", lambda e: e.dma_start(out=pk_d, in_=KFo), reads=[RD("KFo")], dma="o_pk", final=True)
                T.op("dve", lambda e: e.bn_stats(out=bnst[:], in_=VP), reads=[RD("GEL", 1)], writes=[R("bnst")])
                T.op("dve", lambda e: e.bn_aggr(out=st[:, 0:2], in_=bnst[:]), reads=[R("bnst")], writes=[R("st_mv")])
                rstd_ops(st[:, 1:2], st[:, 2:3], neghalf[:, 0:1], 1.0, EPS, [R("st_mv")], [R("st_rv")])
                T.op("dve", lambda e: e.tensor_scalar(out=VP, in0=VP, scalar1=st[:, 0:1], scalar2=st[:, 2:3], op0=ALU.subtract, op1=ALU.mult),
                     reads=[RD("GEL", 1), R("st_mv"), R("st_rv")], writes=[RD("GEL", 1)])
                T.op("dve", lambda e: e.tensor_tensor(out=VP, in0=VP, in1=GAV, op=ALU.mult), reads=[RD("GEL", 1), RD("GAV")], writes=[RD("GEL", 1)])
                T.op("dve", lambda e: e.tensor_tensor(out=VAB, in0=VP, in1=BAV, op=ALU.add), reads=[RD("GEL", 1), RD("BAV")], writes=[RD("VAB")])
                if outt:
                    T.op("dve", lambda e: e.tensor_tensor(out=VAF, in0=VP, in1=BAV, op=ALU.add), reads=[RD("GEL", 1), RD("BAV")], writes=[RD("VAF")])
                    T.op("sp", lambda e: e.dma_start(out=(sa_d if S else pa_d), in_=VAF), reads=[RD("VAF")], dma=("o_va", int(S)), final=True)

            def B2(t):
                S = (t == NT - 1)
                qb = t % 2
                mb = t % 2

                def tr2(e):
                    for b in range(5):
                        ins = e.transpose(out=psb[0][:, b * 128:(b + 1) * 128], in_=QR[:, b * 128:(b + 1) * 128], identity=identb[:])
                    return ins
                T.op("pe", tr2, reads=[RD("QR"), R("identb")], writes=[R("ps", 0)])
                T.op("act", lambda e: e.activation(out=QTs[qb][:, :, :], in_=psb[0][:, 0:512].rearrange("p (b t) -> p b t", b=4), func=AF.Copy),
                     reads=[R("ps", 0)], writes=[RD("QT", qb)])
                T.op("act", lambda e: e.activation(out=KT[:, t, :], in_=psb[0][:, 512:640], func=AF.Copy), reads=[R("ps", 0)], writes=[RD("KT", t)])
                WSX = WBD if S else WST

                def mmc(e):
                    for h in range(8):
                        ins = e.matmul(ps[7][:, h * 64:(h + 1) * 64], lhsT=WSX[:, h, :], rhs=VAB[:, h * 64:(h + 1) * 64],
                                       start=(h == 0), stop=(h == 7), skip_group_check=True)
                    return ins
                T.op("pe", mmc, reads=[RD("WBD") if S else RD("WST"), RD("VAB")], writes=[R("ps", 7)])
                BSX = BSS if S else BST
                T.op("dve", lambda e: e.tensor_tensor(out=TMP.rearrange("p (h d) -> p h d", h=8), in0=ps[7][:, :].rearrange("p (h d) -> p h d", h=8),
                                                      in1=BSX.unsqueeze(2).to_broadcast([128, 8, 64]), op=ALU.add),
                     reads=[R("ps", 7)] + (BSSR if S else [RD("BST")]), writes=[RD("TMP")])
                T.op("dve", lambda e: e.tensor_tensor(out=A2b, in0=TMP, in1=GEL[:, 0:512], op=ALU.mult), reads=[RD("TMP"), RD("GEL", 0)], writes=[RD("A2")])

            def B2c(t):
                mb = t % 2
                T.op("act", lambda e: e.activation(out=MIXs[mb][:, 0:512], in_=A2b, func=AF.Square, accum_out=st[:, 32:33]), reads=[RD("A2")], writes=[RD("MIXa", mb), R("st_sa")])
                rstd_ops(st[:, 32:33], st[:, 33:34], neghalf[:, 0:1], 1.0 / 512, EPS, [R("st_sa")], [R("st_ra")])
                T.op("dve", lambda e: e.scalar_tensor_tensor(out=MIXs[mb][:, 0:512], in0=A2b, scalar=st[:, 33:34], in1=GOA, op0=ALU.mult, op1=ALU.mult),
                     reads=[RD("A2"), R("st_ra"), RD("GOA")], writes=[RD("MIXa", mb)])

            def C1(t):
                qb = t % 2
                blocks = ([(t - 1, mprev, "mprev")] if t > 0 else []) + [(t, mcur, "mcur")]
                for kh in range(2):
                    for bi, (tk, mk, mkn) in enumerate(blocks):
                        bank = 5 + bi
                        idx = kh * 2 + bi
                        T.op("pe", lambda e, kh=kh, tk=tk, bank=bank: e.matmul(ps[bank][:, :], lhsT=KT[64 * kh:64 * kh + 64, tk, :],
                                                                                rhs=QTs[qb][64 * kh:64 * kh + 64, :, :], start=True, stop=True),
                             reads=[RD("KT", tk), RD("QT", qb)], writes=[R("ps", bank)])
                        T.op("act", lambda e, idx=idx, bank=bank: e.activation(out=PT[:, idx, :], in_=ps[bank][:, :], func=AF.Exp, scale=0.125),
                             reads=[R("ps", bank)], writes=[RD("PT", idx)])
                        T.op("dve", lambda e, idx=idx, mk=mk: e.tensor_tensor(out=PT[:, idx, :].rearrange("p (b t) -> p b t", b=4),
                                                                             in0=PT[:, idx, :].rearrange("p (b t) -> p b t", b=4),
                                                                             in1=mk[:, :].unsqueeze(1).to_broadcast([128, 4, 128]), op=ALU.mult),
                             reads=[RD("PT", idx), R(mkn)], writes=[RD("PT", idx)])

            def C2(t):
                mb = t % 2
                blocks = ([(t - 1,)] if t > 0 else []) + [(t,)]

                def mmpv(e):
                    first = True
                    for h in range(8):
                        kh, b = h // 4, h % 4
                        for bi, (tk,) in enumerate(blocks):
                            ins = e.matmul(ps[7][:, h * 64:(h + 1) * 64], lhsT=PT[:, kh * 2 + bi, b * 128:(b + 1) * 128],
                                           rhs=VAUG[:, tk, kh, 0:64], start=first, stop=False, skip_group_check=True)
                            first = False
                    first = True
                    for h in range(8):
                        kh, b = h // 4, h % 4
                        for bi, (tk,) in enumerate(blocks):
                            ins = e.matmul(ps[4][:, 256 + h:257 + h], lhsT=PT[:, kh * 2 + bi, b * 128:(b + 1) * 128],
                                           rhs=VAUG[:, tk, kh, 64:65], start=first, stop=False, skip_group_check=True)
                            first = False
                    return ins
                T.op("pe", mmpv, reads=[RD("PT", i) for i in range(4)] + [RD("VAUG1")] + [RD("VAUG", tk) for (tk,) in blocks], writes=[R("ps", 7), R("ps4d"), R("ps", 4)])
                T.op("dve", lambda e: e.tensor_tensor(out=st[:, 40:48], in0=ps[4][:, 256:264], in1=esink[:, :], op=ALU.add),
                     reads=[R("ps4d"), R("esink")], writes=[R("st_den")])
                T.op("dve", lambda e: e.reciprocal(out=st[:, 48:56], in_=st[:, 40:48]), reads=[R("st_den")], writes=[R("st_rden")])
                T.op("dve", lambda e: e.tensor_tensor(out=OB.rearrange("p (h d) -> p h d", h=8), in0=ps[7][:, :].rearrange("p (h d) -> p h d", h=8),
                                                      in1=st[:, 48:56].unsqueeze(2).to_broadcast([128, 8, 64]), op=ALU.mult),
                     reads=[R("ps", 7), R("st_rden")], writes=[RD("OB")])
                fin_b(t)

            def fin_b(t):
                mb = t % 2
                T.op("act", lambda e: e.activation(out=MIXs[mb][:, 512:1024], in_=OB, func=AF.Square, accum_out=st[:, 34:35]), reads=[RD("OB")], writes=[RD("MIXb", mb), R("st_sb")])
                rstd_ops(st[:, 34:35], st[:, 35:36], neghalf[:, 0:1], 1.0 / 512, EPS, [R("st_sb")], [R("st_rb")])
                T.op("dve", lambda e: e.scalar_tensor_tensor(out=MIXs[mb][:, 512:1024], in0=OB, scalar=st[:, 35:36], in1=GOB, op0=ALU.mult, op1=ALU.mult),
                     reads=[RD("OB"), R("st_rb"), RD("GOB")], writes=[RD("MIXb", mb)])

            def CSamp(t):
                qb = t % 2
                QTq = QTs[qb]

                def mmsc(e):
                    for n in range(16):
                        for kh in range(2):
                            ins = e.matmul(ps[1 + kh][:, n * 32:n * 32 + 32],
                                           lhsT=KCT[64 * kh:64 * kh + 64, n, :], rhs=QTq[64 * kh:64 * kh + 64, :, 8 * n:8 * n + 8],
                                           start=(n == 0), stop=False, skip_group_check=True)
                    return ins
                T.op("pe", mmsc, reads=[RD("KCT"), RD("QT", qb)], writes=[R("ps", 1), R("ps", 2)])
                for half in range(2):
                    T.op("act", lambda e, half=half: e.activation(out=PTC[:, half * 512:(half + 1) * 512], in_=ps[1 + half][:, :], func=AF.Exp, scale=0.125),
                         reads=[R("ps", 1 + half)], writes=[RD("PT", 2 + half)])
                T.op("dve", lambda e: e.tensor_tensor(out=PTC.rearrange("p (a i) -> p a i", i=8), in0=PTC.rearrange("p (a i) -> p a i", i=8),
                                                      in1=mc[:, :].unsqueeze(1).to_broadcast([128, 128, 8]), op=ALU.mult),
                     reads=[RD("PT", 2), RD("PT", 3), R("mc")], writes=[RD("PT", 2), RD("PT", 3)])
                for kh in range(2):
                    T.op("pe", lambda e, kh=kh: e.matmul(ps[3 + kh][:, :], lhsT=KT[64 * kh:64 * kh + 64, t, :],
                                                         rhs=QTq[64 * kh:64 * kh + 64, :, :].rearrange("p b (n i) -> p n b i", n=16), start=True, stop=True),
                         reads=[RD("KT", t), RD("QT", qb)], writes=[R("ps", 3 + kh)])
                    T.op("act", lambda e, kh=kh: e.activation(out=PT[:, kh, :], in_=ps[3 + kh][:, :], func=AF.Exp, scale=0.125),
                         reads=[R("ps", 3 + kh)], writes=[RD("PT", kh)])
                    T.op("dve", lambda e, kh=kh: e.tensor_tensor(out=PT[:, kh, :].rearrange("p (n b i) -> p n b i", n=16, b=4), in0=PT[:, kh, :].rearrange("p (n b i) -> p n b i", n=16, b=4),
                                                                in1=mnew[:, :].rearrange("p (n i) -> p n i", n=16).unsqueeze(2).to_broadcast([128, 16, 4, 8]), op=ALU.mult),
                         reads=[RD("PT", kh), R("mnew")], writes=[RD("PT", kh)])
                PTC5 = PTC.rearrange("p (k n b i) -> p k n b i", n=16, k=2, b=4)

                def mmot(e):
                    for kh in range(2):
                        for n in range(16):
                            e.matmul(ps[1 + kh][0:65, n * 32:(n + 1) * 32], lhsT=VC[:, n, kh, :], rhs=PTC[:, kh * 512 + n * 32:kh * 512 + (n + 1) * 32],
                                     start=(n == 0), stop=False, skip_group_check=True)
                        ins = e.matmul(ps[1 + kh][0:65, :], lhsT=VAUG[:, t, kh, :], rhs=PT[:, kh, :], start=False, stop=True, skip_group_check=True)
                    return ins
                T.op("pe", mmot, reads=[RD("PT", i) for i in range(4)] + [RD("VC", 0), RD("VC", 1), RD("VC1"), RD("VAUG", t), RD("VAUG1")], writes=[R("ps", 1), R("ps", 2)])
                for kh in range(2):
                    T.op("act", lambda e, kh=kh: e.activation(out=OTS[0:65, kh, :].rearrange("p (b n i) -> p b n i", b=4, n=16),
                                                              in_=ps[1 + kh][0:65, :].rearrange("p (n b i) -> p b n i", n=16, b=4), func=AF.Copy),
                         reads=[R("ps", 1 + kh)], writes=[RD("KC")])

                def trot(e):
                    for bank in range(2):
                        for hh in range(4):
                            h = bank * 4 + hh
                            kh, b = h // 4, h % 4
                            ins = e.transpose(out=ps[5 + bank][:, hh * 65:(hh + 1) * 65], in_=OTS[0:65, kh, b * 128:(b + 1) * 128], identity=identf[0:65, 0:65])
                    return ins
                T.op("pe", trot, reads=[RD("KC"), R("identf")], writes=[R("ps", 5), R("ps", 6)])
                for bank in range(2):
                    o3 = ps[5 + bank][:, 0:260].rearrange("p (h d) -> p h d", h=4)
                    T.op("dve", lambda e, bank=bank, o3=o3: e.tensor_tensor(out=st[:, 40 + bank * 4:44 + bank * 4].unsqueeze(2), in0=o3[:, :, 64:65],
                                                                            in1=esink[:, bank * 4:bank * 4 + 4].unsqueeze(2), op=ALU.add),
                         reads=[R("ps", 5 + bank), R("esink")], writes=[R("st_den")])
                T.op("dve", lambda e: e.reciprocal(out=st[:, 48:56], in_=st[:, 40:48]), reads=[R("st_den")], writes=[R("st_rden")])
                for bank in range(2):
                    o3 = ps[5 + bank][:, 0:260].rearrange("p (h d) -> p h d", h=4)
                    T.op("dve", lambda e, bank=bank, o3=o3: e.tensor_tensor(out=OB[:, bank * 256:(bank + 1) * 256].rearrange("p (h d) -> p h d", h=4), in0=o3[:, :, 0:64],
                                                                            in1=st[:, 48 + bank * 4:52 + bank * 4].unsqueeze(2).to_broadcast([128, 4, 64]), op=ALU.mult),
                         reads=[R("ps", 5 + bank), R("st_rden")], writes=[RD("OB")])
                fin_b(t)

            def D1(t):
                mb = t % 2

                def tr3(e):
                    for k in range(8):
                        ins = e.transpose(out=psb[0][:, k * 128:(k + 1) * 128], in_=MIXs[mb][:, k * 128:(k + 1) * 128], identity=identb[:])
                    return ins
                T.op("pe", tr3, reads=[RD("MIXa", mb), RD("MIXb", mb), R("identb")], writes=[R("ps", 0)])
                T.op("act", lambda e: e.activation(out=MIXT, in_=psb[0][:, :], func=AF.Copy), reads=[R("ps", 0)], writes=[RD("MIXT")])

            def D1b(t):
                for hf in range(2):
                    def mmo(e, hf=hf):
                        for k in range(8):
                            ins = e.matmul(ps[5 + hf][:, :], lhsT=MIXT3[:, k, :], rhs=WO[:, k, hf * 512:(hf + 1) * 512], start=(k == 0), stop=(k == 7))
                        return ins
                    T.op("pe", mmo, reads=[RD("MIXT"), RD("WO")], writes=[R("ps", 5 + hf)])

            def D2(t):
                for hf in range(2):
                    T.op("dve", lambda e, hf=hf: e.tensor_tensor(out=X[:, t, hf * 512:(hf + 1) * 512], in0=ps[5 + hf][:, :], in1=X[:, t, hf * 512:(hf + 1) * 512], op=ALU.add),
                         reads=[R("ps", 5 + hf), R("X", t)], writes=[R("X", t)])

            def F2stats(t):
                T.op("act", lambda e: e.activation(out=JUNKM, in_=X[:, t, :], func=AF.Square, accum_out=ssq[:, t:t + 1]),
                     reads=[R("X", t)], writes=[RD("JUNKM"), R("ssq", t)])

            if ntiles > 0:
                A1(0)
                ZA(0)
                ZB(0)
                if ntiles > 1:
                    A1(1)
                B1a(0)
                B1a2(0)
            for i in range(ntiles + 1):
                cur = i < ntiles
                prev = i >= 1
                nxt = i + 1 < ntiles
                if cur and i >= 1:
                    B1a2(i)
                if prev and (i - 1) != NT - 1:
                    C1(i - 1)
                if prev:
                    B2c(i - 1)
                if nxt:
                    ZA(i + 1)
                if cur:
                    B1g(i)
                if i >= 2:
                    F2stats(i - 2)
                if prev:
                    if (i - 1) == NT - 1:
                        CSamp(i - 1)
                    else:
                        C2(i - 1)
                if nxt:
                    ZB(i + 1, 0)
                if cur:
                    B1b(i)
                if prev:
                    D1(i - 1)
                if nxt:
                    ZB(i + 1, 1)
                if prev:
                    D1b(i - 1)
                    D2(i - 1)
                if nxt:
                    B1a(i + 1)
                if cur:
                    B2(i)
                if i + 2 < ntiles:
                    A1(i + 2)
            if ntiles >= 1:
                F2stats(ntiles - 1)
            if ntiles == NT:
                rstd_ops(ssq[:, :], rstd[:, :], neghalf[:, :], 1.0 / DM, EPS, [R("ssq", t) for t in range(NT)], [R("rstd", t) for t in range(NT)])

        def ple_phase():
            T.new_phase()
            lay = Lay()
            PET = lay.take(2 * TK * 2, BF16).rearrange("p (k t) -> p k t", k=2)
            PST = lay.take(NT * 256 * 2, BF16).rearrange("p (t c) -> p t c", t=NT)
            HN = [lay.take(DM * 2, BF16) for _ in range(2)]
            HNT = [lay.take(DM * 2, BF16) for _ in range(2)]
            JUNK = lay.take(DM * 2, BF16)
            SIG = [lay.take(DM * 4) for _ in range(2)]
            TP = [lay.take(DM * 4) for _ in range(2)]
            assert lay.off <= ZOFF, lay.off
            gs = load_gain(gp_d)
            T.op("pool", lambda e: e.dma_start(out=PST[:], in_=p_d.rearrange("(t q) c -> q t c", q=128)), writes=[RD("PST")], dma="c_pst")
            for t in range(NT):
                bank = t % 2

                def trp(e, t=t, bank=bank):
                    for kk in range(2):
                        ins = e.transpose(out=psb[bank][:, kk * 128:(kk + 1) * 128], in_=PST[:, t, kk * 128:(kk + 1) * 128], identity=identb[:])
                    return ins
                T.op("pe", trp, reads=[RD("PST"), R("identb")], writes=[R("ps", bank)])
                T.op("act", lambda e, t=t, bank=bank: e.activation(out=PET[:, :, t * 128:(t + 1) * 128], in_=psb[bank][:, 0:256].rearrange("p (k c) -> p k c", k=2), func=AF.Copy),
                     reads=[R("ps", bank)], writes=[RD("PET", t)])
            cnt = [0]

            def PA1n(t):
                b2 = t % 2
                T.op("dve", lambda e: e.scalar_tensor_tensor(out=HN[b2], in0=X[:, t, :], scalar=rstd[:, t:t + 1], in1=G[gs][:], op0=ALU.mult, op1=ALU.mult),
                     reads=[R("X", t), R("rstd", t), R("G", gs)], writes=[RD("HN", b2)])

            def PA1t(t):
                b2 = t % 2

                def tr1(e):
                    for k in range(8):
                        ins = e.transpose(out=psb[b2][:, k * 128:(k + 1) * 128], in_=HN[b2][:, k * 128:(k + 1) * 128], identity=identb[:])
                    return ins
                T.op("pe", tr1, reads=[RD("HN", b2), R("identb")], writes=[R("ps", b2)])

            def PA2(t):
                b2 = t % 2
                T.op("act", lambda e: e.activation(out=HNT[b2], in_=psb[b2][:, :], func=AF.Copy), reads=[R("ps", b2)], writes=[RD("HNT", b2)])

            def PB(t):
                b2 = t % 2
                H3 = HNT[b2].rearrange("p (k t) -> p k t", k=8)
                for hf in range(2):
                    pr = cnt[0] % 3
                    cnt[0] += 1
                    bgate, bproj = 2 + 2 * pr, 3 + 2 * pr

                    def mmg(e, hf=hf, bgate=bgate, bproj=bproj):
                        for k in range(8):
                            e.matmul(ps[bgate][:, :], lhsT=H3[:, k, :], rhs=WPG[:, k, hf * 512:(hf + 1) * 512], start=(k == 0), stop=(k == 7))
                        for kk in range(2):
                            ins = e.matmul(ps[bproj][:, :], lhsT=PET[:, kk, t * 128:(t + 1) * 128], rhs=WPP[:, kk, hf * 512:(hf + 1) * 512], start=(kk == 0), stop=(kk == 1))
                        return ins
                    T.op("pe", mmg, reads=[RD("HNT", b2), R("WPG"), R("WPP"), RD("PET", t)], writes=[R("ps", bgate), R("ps", bproj)])
                    T.op("act", lambda e, hf=hf, bgate=bgate: e.activation(out=SIG[b2][:, hf * 512:(hf + 1) * 512], in_=ps[bgate][:, :], func=AF.Sigmoid),
                         reads=[R("ps", bgate)], writes=[RD("SIG", b2, hf)])
                    T.op("dve", lambda e, hf=hf, bproj=bproj: e.tensor_tensor(out=TP[b2][:, hf * 512:(hf + 1) * 512], in0=SIG[b2][:, hf * 512:(hf + 1) * 512],
                                                                          in1=ps[bproj][:, :], op=ALU.mult),
                         reads=[RD("SIG", b2, hf), R("ps", bproj)], writes=[RD("TP", b2, hf)])
                T.op("dve", lambda e: e.tensor_tensor(out=X[:, t, :], in0=X[:, t, :], in1=TP[b2], op=ALU.add),
                     reads=[R("X", t), RD("TP", b2, 0), RD("TP", b2, 1)], writes=[R("X", t)])
                T.op("sp", lambda e: e.dma_start(out=y_d[t * 128:(t + 1) * 128, :], in_=X[:, t, :]), reads=[R("X", t)], dma="o_y", final=True)

            PA1n(0)
            PA1t(0)
            PA2(0)
            PA1n(1)
            PA1t(1)
            PA1n(2)
            PA1t(2)
            for t in range(NT):
                if t + 1 < NT:
                    PA2(t + 1)
                if t + 3 < NT:
                    PA1n(t + 3)
                PB(t)
                if t + 3 < NT:
                    PA1t(t + 3)

        def store_y():
            for t in range(NT):
                T.op("sp", lambda e, t=t: e.dma_start(out=y_d[t * 128:(t + 1) * 128, :], in_=X[:, t, :]), reads=[R("X", t)], dma="o_y", final=True)

        ffn_phase("f1", g1_d, w1g_d, w1u_d, w1d_d, after_prologue=prefetch_M, first=True)
        if upto == "F1":
            store_y()
        elif upto.startswith("M:"):
            _, a_, b_ = upto.split(":")
            mix_phase(int(a_), float(b_))
            store_y()
        else:
            mix_phase()
            if upto == "M":
                store_y()
            else:
                ffn_phase("f2", g2_d, w2g_d, w2u_d, w2d_d, have_stats=True, after_prologue=prefetch_P)
                if upto == "F2":
                    store_y()
                else:
                    ple_phase()
        if debug:
            for nm, ap_, res_ in DBG:
                dd = nc.dram_tensor("dbg_" + nm, list(ap_.shape), ap_.dtype, kind="ExternalOutput").ap()
                T.op("sp", lambda e, dd=dd, ap_=ap_: e.dma_start(out=dd, in_=ap_), reads=[res_], dma="o_dbg", final=True)
            dst = nc.dram_tensor("dbg_st", [128, 64], F32, kind="ExternalOutput").ap()
            T.op("sp", lambda e: e.dma_start(out=dst, in_=st[:]), reads=[R("st_rden"), R("st_rb"), R("st_ra"), R("st_rh"), R("st_rv"), R("st_mv")], dma="o_dbg", final=True)
            dcs = nc.dram_tensor("dbg_CS", [128, 2 * NT * 32], F32, kind="ExternalOutput").ap()
            T.op("sp", lambda e: e.dma_start(out=dcs, in_=CS[:].rearrange("p c t j -> p (c t j)")), reads=[R("CS")], dma="o_dbg", final=True)
        T.emit(nc, es)
    return nc


def _consts():
    ident = np.eye(128, dtype=np.float32)
    pos = np.zeros((128, NT), np.float32)
    for t in range(16):
        pos[:, t] = t * 128 + np.arange(128)
    pos[:, 16] = 16384 + (np.arange(128) % 8)
    half = 32
    invf = (10000.0 ** (-np.arange(half, dtype=np.float32) / half)).astype(np.float32)
    invf = np.broadcast_to(invf[None, :], (128, 32)).copy()
    s = np.arange(128)[:, None]
    q = np.arange(128)[None, :]
    mcur = (s <= q).astype(np.float32)
    mprev = (s > q).astype(np.float32)
    mc = (s > np.arange(8)[None, :]).astype(np.float32)
    mnew = ((s // 8 == q // 8) & (s % 8 <= q % 8)).astype(np.float32)
    sel = (np.arange(128)[None, :] % 8 == np.arange(8)[:, None]).astype(np.float32)
    return dict(ident=ident, pos=pos, invf=invf, mcur=mcur, mprev=mprev, mc=mc, mnew=mnew, sel=sel)


def make_in_maps(inp):
    f = lambda a: np.ascontiguousarray(np.asarray(a, dtype=np.float32))
    shared = dict(
        w1g=f(inp["w_ffn1_gate"][0]), w1u=f(inp["w_ffn1_up"][0]), w1d=f(inp["w_ffn1_down"][0]),
        w2g=f(inp["w_ffn2_gate"][0]), w2u=f(inp["w_ffn2_up"][0]), w2d=f(inp["w_ffn2_down"][0]),
        win=f(inp["w_in"][0]), wo=f(inp["w_o"][0]), wpg=f(inp["w_ple_gate"][0]), wpp=f(inp["w_ple_proj"][0]),
        g1=f(inp["g_ffn1"]), gm=f(inp["g_mix"]), g2=f(inp["g_ffn2"]), gp=f(inp["g_ple"]),
        gav=f(inp["g_a_v"]), bav=f(inp["b_a_v"]), goa=f(inp["g_out_a"]), gob=f(inp["g_out_b"]),
        gq=f(inp["g_q"]), gk=f(inp["g_k"]), sinks=f(inp["sinks"]),
        ws=f(inp["w_s"][0]), bs=f(inp["b_s"][0]),
    )
    shared.update(_consts())
    xp, xs = f(inp["x_prompt"]), f(inp["x_sample"])
    pp, psm = f(inp["p_prompt"][0]), f(inp["p_sample"][0])
    ck, cv = f(inp["cache_k_win"][0]), f(inp["cache_v_win"][0])
    maps = []
    for c in range(NCORES):
        sl = slice(16 * c, 16 * c + 16)
        m = dict(shared)
        m["x"] = np.concatenate([xp[c], xs[sl].reshape(128, DM)], axis=0)
        m["p"] = np.concatenate([pp[c], psm[sl].reshape(128, 256)], axis=0)
        m["ck"] = ck[sl].reshape(16, 128, 128)
        m["cv"] = cv[sl].reshape(16, 128, 128)
        maps.append(m)
    return maps


_NC_CACHE = {}


def kernel(**inputs):
    if "nc" not in _NC_CACHE:
        _NC_CACHE["nc"] = build_nc("P")
    nc = _NC_CACHE["nc"]
    maps = make_in_maps(inputs)
    res = run_bass_kernel_spmd(nc, maps, core_ids=list(range(NCORES)))
    rs = res.results
    y = np.stack([r["y"] for r in rs])
    yp = np.ascontiguousarray(y[:, 0:2048, :])
    ys = np.ascontiguousarray(y[:, 2048:, :].reshape(128, 8, DM))
    pk = np.stack([r["pk"] for r in rs]).reshape(1, 8, 128, 2, 64)
    pv = np.stack([r["pv"] for r in rs]).reshape(1, 8, 128, 2, 64)
    pa = np.stack([r["pa"] for r in rs]).reshape(1, 8, 128, 512)
    sk = np.concatenate([r["sk"] for r in rs]).reshape(1, 128, 128, 2, 64)
    sv = np.concatenate([r["sv"] for r in rs]).reshape(1, 128, 128, 2, 64)
    sa = np.concatenate([r["sa"] for r in rs]).reshape(1, 128, 8, 512)
    return (yp, ys, pk.astype(np.float32), pv.astype(np.float32), pa.astype(np.float32),
            sk.astype(np.float32), sv.astype(np.float32), sa.astype(np.float32))
```
